# Optimizing a Trainium2 kernel written in Bass

```python
import jax, jax.numpy as jnp
from jax import lax
import numpy as np

D_MODEL = 1024
BATCH = 4
SEQ = 8192
DEPTH = 1

CHUNK = 64
HEAD_DIM = 64
N_HEADS_A = 8
N_HEADS_B = 8
D_A = N_HEADS_A * HEAD_DIM
D_B = N_HEADS_B * HEAD_DIM
D_MIX = D_A + D_B
N_LEFT_CHUNKS = 8
BAND_CHUNKS = N_LEFT_CHUNKS + 1
REL_CLIP = 128
Q_BLOCK = 128
D_FF = 2816
CONV_WIDTH = 3
LN_EPS = 1e-5
DEEPNORM_ALPHA = (2.0 * DEPTH) ** 0.25
DEEPNORM_BETA = (8.0 * DEPTH) ** -0.25
PROJ_COLS = 3 * D_A + N_HEADS_A + 3 * D_B
SPLITS = (D_A, 2 * D_A, 3 * D_A, 3 * D_A + N_HEADS_A,
          3 * D_A + N_HEADS_A + D_B, 3 * D_A + N_HEADS_A + 2 * D_B)

kernel_name = "hymba_style_fox_chunkband_convffn_deepnorm"


def layer_norm(x, g, b):
    xf = x.astype(jnp.float32)
    mu = jnp.mean(xf, axis=-1, keepdims=True)
    var = jnp.mean(jnp.square(xf - mu), axis=-1, keepdims=True)
    y = (xf - mu) * lax.rsqrt(var + LN_EPS)
    return (y * g.astype(jnp.float32) + b.astype(jnp.float32)).astype(x.dtype)


def split_heads(t, n_heads):
    b, s, _ = t.shape
    return t.reshape(b, s, n_heads, HEAD_DIM).transpose(0, 2, 1, 3)


def forgetting_attention(q, k, v, log_f):
    b, h, s, d = q.shape
    n_blk = s // Q_BLOCK
    cum = jnp.cumsum(log_f.astype(jnp.float32), axis=-1)
    q_blocks = q.reshape(b, h, n_blk, Q_BLOCK, d).transpose(2, 0, 1, 3, 4)
    c_blocks = cum.reshape(b, h, n_blk, Q_BLOCK).transpose(2, 0, 1, 3)
    starts = jnp.arange(n_blk, dtype=jnp.int32) * Q_BLOCK
    k_pos = jnp.arange(s, dtype=jnp.int32)
    scale = d ** -0.5

    def one_block(args):
        qb, cb, start = args
        logits = jnp.einsum('bhqd,bhkd->bhqk', qb, k).astype(jnp.float32) * scale
        logits = logits + cb[..., :, None] - cum[..., None, :]
        q_pos = start + jnp.arange(Q_BLOCK, dtype=jnp.int32)
        causal = k_pos[None, :] <= q_pos[:, None]
        logits = jnp.where(causal, logits, -jnp.inf)
        p = jax.nn.softmax(logits, axis=-1)
        return jnp.einsum('bhqk,bhkd->bhqd', p.astype(v.dtype), v)

    out = lax.map(one_block, (q_blocks, c_blocks, starts))
    return out.transpose(1, 2, 0, 3, 4).reshape(b, h, s, d)


def chunked_band_attention(q, k, v, rel_bias):
    b, h, s, d = q.shape
    n_c = s // CHUNK
    band = BAND_CHUNKS * CHUNK
    qc = q.reshape(b, h, n_c, CHUNK, d)
    pad = ((0, 0), (0, 0), (N_LEFT_CHUNKS, 0), (0, 0), (0, 0))
    kp = jnp.pad(k.reshape(b, h, n_c, CHUNK, d), pad)
    vp = jnp.pad(v.reshape(b, h, n_c, CHUNK, d), pad)
    band_idx = np.arange(n_c)[:, None] + np.arange(BAND_CHUNKS)[None, :]
    k_band = kp[:, :, band_idx].reshape(b, h, n_c, band, d)
    v_band = vp[:, :, band_idx].reshape(b, h, n_c, band, d)
    dist = (N_LEFT_CHUNKS * CHUNK + np.arange(CHUNK)[:, None]) - np.arange(band)[None, :]
    rel_idx = np.clip(dist, -REL_CLIP, REL_CLIP) + REL_CLIP
    bias = rel_bias.astype(jnp.float32)[:, rel_idx]
    valid = (np.arange(n_c)[:, None] + np.arange(band)[None, :] // CHUNK) >= N_LEFT_CHUNKS
    logits = jnp.einsum('bhcqd,bhckd->bhcqk', qc, k_band).astype(jnp.float32) * (d ** -0.5)
    logits = logits + bias[None, :, None, :, :]
    logits = jnp.where(valid[None, None, :, None, :], logits, -jnp.inf)
    p = jax.nn.softmax(logits, axis=-1)
    out = jnp.einsum('bhcqk,bhckd->bhcqd', p.astype(v.dtype), v_band)
    return out.reshape(b, h, s, d)


def token_mixer(x, w_in, b_forget, rel_bias, w_out):
    b, s, _ = x.shape
    proj = x @ w_in
    q_a, k_a, v_a, f_a, q_b, k_b, v_b = jnp.split(proj, SPLITS, axis=-1)
    log_f = jax.nn.log_sigmoid((f_a + b_forget).astype(jnp.float32)).transpose(0, 2, 1)
    y_a = forgetting_attention(split_heads(q_a, N_HEADS_A), split_heads(k_a, N_HEADS_A),
                               split_heads(v_a, N_HEADS_A), log_f)
    y_b = chunked_band_attention(split_heads(q_b, N_HEADS_B), split_heads(k_b, N_HEADS_B),
                                 split_heads(v_b, N_HEADS_B), rel_bias)
    y = jnp.concatenate([y_a, y_b], axis=1)
    y = y.transpose(0, 2, 1, 3).reshape(b, s, D_MIX)
    return y @ w_out


def conv_ffn(x, w_up, conv_w, conv_b, w_down):
    s = x.shape[1]
    u = x @ w_up
    u_pad = jnp.pad(u, ((0, 0), (CONV_WIDTH - 1, 0), (0, 0)))
    u = sum(u_pad[:, j:j + s, :] * conv_w[j] for j in range(CONV_WIDTH)) + conv_b
    value, gate = jnp.split(u, 2, axis=-1)
    return (value * jax.nn.gelu(gate)) @ w_down


def setup_inputs(seed: int = 0) -> dict:
    key = jax.random.key(seed)
    ks = jax.random.split(key, 16)
    f32 = jnp.float32
    x = jax.random.normal(ks[0], (BATCH, SEQ, D_MODEL), f32)
    col_scale = np.ones((PROJ_COLS,), np.float32)
    col_scale[2 * D_A:3 * D_A] = DEEPNORM_BETA
    col_scale[SPLITS[5]:] = DEEPNORM_BETA
    w_in = jax.random.normal(ks[1], (DEPTH, D_MODEL, PROJ_COLS), f32) * (D_MODEL ** -0.5) * jnp.asarray(col_scale)
    b_forget = jax.random.uniform(ks[2], (DEPTH, N_HEADS_A), f32, 2.0, 5.0)
    rel_bias = jax.random.normal(ks[3], (DEPTH, N_HEADS_B, 2 * REL_CLIP + 1), f32) * 0.5
    w_out = jax.random.normal(ks[4], (DEPTH, D_MIX, D_MODEL), f32) * (D_MIX ** -0.5) * DEEPNORM_BETA
    ln1_g = 1.0 + 0.05 * jax.random.normal(ks[5], (DEPTH, D_MODEL), f32)
    ln1_b = 0.02 * jax.random.normal(ks[6], (DEPTH, D_MODEL), f32)
    w_up = jax.random.normal(ks[7], (DEPTH, D_MODEL, 2 * D_FF), f32) * (D_MODEL ** -0.5)
    conv_w = jax.random.normal(ks[8], (DEPTH, CONV_WIDTH, 2 * D_FF), f32) * (CONV_WIDTH ** -0.5)
    conv_b = 0.02 * jax.random.normal(ks[9], (DEPTH, 2 * D_FF), f32)
    w_down = jax.random.normal(ks[10], (DEPTH, D_FF, D_MODEL), f32) * (D_FF ** -0.5) * DEEPNORM_BETA
    ln2_g = 1.0 + 0.05 * jax.random.normal(ks[11], (DEPTH, D_MODEL), f32)
    ln2_b = 0.02 * jax.random.normal(ks[12], (DEPTH, D_MODEL), f32)
    return {"x": x, "w_in": w_in, "b_forget": b_forget, "rel_bias": rel_bias, "w_out": w_out,
            "ln1_g": ln1_g, "ln1_b": ln1_b, "w_up": w_up, "conv_w": conv_w, "conv_b": conv_b,
            "w_down": w_down, "ln2_g": ln2_g, "ln2_b": ln2_b}


def reference(x, w_in, b_forget, rel_bias, w_out, ln1_g, ln1_b, w_up, conv_w, conv_b,
              w_down, ln2_g, ln2_b):
    for l in range(DEPTH):
        mix = token_mixer(x, w_in[l], b_forget[l], rel_bias[l], w_out[l])
        x = layer_norm(DEEPNORM_ALPHA * x + mix, ln1_g[l], ln1_b[l])
        ffn = conv_ffn(x, w_up[l], conv_w[l], conv_b[l], w_down[l])
        x = layer_norm(DEEPNORM_ALPHA * x + ffn, ln2_g[l], ln2_b[l])
    return x
```

```python
import numpy as np
from contextlib import ExitStack
import concourse.bass as bass
import concourse.mybir as mybir
from concourse.bass_utils import run_bass_kernel_spmd

F32, BF16 = mybir.dt.float32, mybir.dt.bfloat16
AF = mybir.ActivationFunctionType
ALU = mybir.AluOpType

D = 1024
SEQ = 8192
HALF = 4096
NB = 64
NG = 9
NQ = 4224
DFF = 2816
NJ = 22
PC = 3080
ALPHA = float(2.0 ** 0.25)
EPS = 1e-5
NEG = -30000.0
DEBUG = {}
STAGE = {"hpA": 4, "hpB": 4, "C": True}


def ginfo(g):
    if g == 0:
        return 31, 128, 0
    return 32 + 4 * (g - 1), 512, 128 + 512 * (g - 1)


class Buf:
    __slots__ = ("name", "w", "r", "sem", "cnt")

    def __init__(self, name):
        self.name = name
        self.w = None
        self.r = {}
        self.sem = None
        self.cnt = 0


class Op:
    __slots__ = ("fn", "waits", "signal", "dma")

    def __init__(self, fn, waits, dma):
        self.fn = fn
        self.waits = waits
        self.signal = False
        self.dma = dma


class Sched:
    ENG = ("pe", "act", "dve", "pool", "sp")

    def __init__(self):
        self.ops = {e: [] for e in self.ENG}
        self.bufs = {}
        self.dmabufs = []

    def B(self, *key):
        b = self.bufs.get(key)
        if b is None:
            b = Buf(key)
            self.bufs[key] = b
        return b

    def op(self, eng, fn, reads=(), writes=(), dma=None):
        deps = set()
        for b in reads:
            if b.w is not None:
                deps.add(b.w)
        for b in writes:
            if b.w is not None:
                deps.add(b.w)
            deps.update(b.r.values())
        seq = len(self.ops[eng])
        if dma is not None:
            if dma.cnt == 0:
                self.dmabufs.append(dma)
            dma.cnt += 16
            ev = ("d", dma, dma.cnt)
            rkey = ("d", id(dma))
        else:
            ev = ("e", eng, seq)
            rkey = eng
        waits = []
        for d in deps:
            if d[0] == "e" and d[1] == eng and (eng == "pe" or dma is not None and False):
                continue
            if d[0] == "e":
                fr = getattr(self, "frozen", None)
                if fr is not None and d[2] <= fr[d[1]]:
                    if not (self.ops[d[1]][d[2]].signal and self.ops[d[1]][d[2]].fn is not None):
                        continue
                else:
                    self.ops[d[1]][d[2]].signal = True
            waits.append(d)
        for b in reads:
            b.r[rkey] = ev
        for b in writes:
            b.w = ev
            b.r = {}
        self.ops[eng].append(Op(fn, waits, dma))
        return ev

    def flush(self, nc, es):
        if not hasattr(self, "sems"):
            self.sems = {e: es.enter_context(nc.semaphore("sem_" + e)) for e in self.ENG}
            self.done = {e: 0 for e in self.ENG}
            self.frozen = {e: -1 for e in self.ENG}
            self.nds = 0
        evs = []
        for e in self.ENG:
            n = len(self.ops[e])
            if n:
                j = n - 1
                while j >= 0 and (self.ops[e][j].fn is None or self.ops[e][j].dma is not None):
                    j -= 1
                if j >= 0:
                    self.ops[e][j].signal = True
                    evs.append(("e", e, j))
        for b in self.dmabufs:
            evs.append(("d", b, b.cnt))
        for e in self.ENG:
            self.ops[e].append(Op(None, list(evs), None))
        for b in self.dmabufs:
            if b.sem is None:
                b.sem = es.enter_context(nc.semaphore("dsem%d" % self.nds))
                self.nds += 1
        cum = {}
        for e in self.ENG:
            c = 0
            arr = []
            for o in self.ops[e]:
                if o.signal and o.dma is None and o.fn is not None:
                    c += 1
                arr.append(c)
            cum[e] = arr
        sems = self.sems
        sched = self
        lo = dict(self.done)
        if not hasattr(self, "waited"):
            self.waited = {e: {} for e in self.ENG}

        def replay(e, eobj):
            waited = sched.waited[e]
            for o in sched.ops[e][lo[e]:]:
                for d in o.waits:
                    if d[0] == "e":
                        sem, val = sems[d[1]], cum[d[1]][d[2]]
                    else:
                        sem, val = d[1].sem, d[2]
                    k = id(sem)
                    if waited.get(k, 0) >= val:
                        continue
                    waited[k] = val
                    eobj.wait_ge(sem, val)
                if o.fn is None:
                    continue
                ins = o.fn(eobj)
                if o.dma is not None:
                    ins.then_inc(o.dma.sem, 16)
                elif o.signal:
                    ins.then_inc(sems[e], 1)

        with nc.Block() as block:
            @block.tensor
            def _(t):
                replay("pe", t)

            @block.scalar
            def _(t):
                replay("act", t)

            @block.vector
            def _(t):
                replay("dve", t)

            @block.gpsimd
            def _(t):
                replay("pool", t)

            @block.sync
            def _(t):
                replay("sp", t)
        for e in self.ENG:
            self.done[e] = len(self.ops[e])
            self.frozen[e] = len(self.ops[e]) - 1


def build_nc():
    nc = bass.Bass("TRN2", target_bir_lowering=False)
    S = Sched()
    B = S.B

    def din(name, shape):
        return nc.dram_tensor(name, shape, F32, kind="ExternalInput").ap()

    xT = din("xT", [D, SEQ])
    xo = din("xo", [NQ, D])
    w_in = din("w_in", [D, PC])
    w_out = din("w_out", [D, D])
    w_up = din("w_up", [D, 2 * DFF])
    w_down = din("w_down", [DFF, D])
    lnp = din("lnp", [4, D])
    convp = din("convp", [128, 44 * 4])
    bfg = din("bfg", [1, 512])
    biasB = din("biasB", [128, 8 * 5 * 128])
    maskA = din("maskA", [1, NG * 64])
    maskB = din("maskB", [1, NG * 8])
    flag = din("flag", [1, 1])
    cst = din("cst", [128, 3 * 128])
    out = nc.dram_tensor("out", [HALF, D], F32, kind="ExternalOutput").ap()

    xTb = nc.dram_tensor("xTb", [D, SEQ], BF16).ap()
    w_in_b = nc.dram_tensor("w_in_b", [D, PC], BF16).ap()
    w_out_b = nc.dram_tensor("w_out_b", [D, D], BF16).ap()
    w_up_b = nc.dram_tensor("w_up_b", [NJ, 128, 8, 256], BF16).ap()
    w_down_b = nc.dram_tensor("w_down_b", [DFF, D], BF16).ap()

    dbg_out = {}
    for k, shp in DEBUG.items():
        dbg_out[k] = nc.dram_tensor("dbg_" + k, list(shp[0]), shp[1], kind="ExternalOutput").ap()

    es = ExitStack()
    with es:
        def sb(name, shape, dt, stack=es):
            return stack.enter_context(nc.sbuf_tensor(name, shape, dt))

        banks = [es.enter_context(nc.psum_tensor("bank%d" % i, [128, 512], F32)) for i in range(7)]
        bankT = es.enter_context(nc.psum_tensor("bankT", [128, 1024], BF16))
        Bbank = [B("bank", i) for i in range(7)]
        BbankT = B("bankT")

        OT = [sb("OT%d" % c, [128, NQ], BF16) for c in range(8)]
        cstf = sb("cstf", [128, 384], F32)
        trib = sb("trib", [128, 128], BF16)
        identb = sb("identb", [128, 128], BF16)
        onesb = sb("onesb", [128, 128], BF16)
        onesf = sb("onesf", [128, 128], F32)
        flagt = sb("flagt", [128, 1], F32)
        utri = cstf[:, 256:384]

        for r in range(8):
            S.op("pool", (lambda e, r=r: e.dma_start(out=w_in_b[r * 128:(r + 1) * 128, :], in_=w_in[r * 128:(r + 1) * 128, :])),
                 writes=[B("w_in_b")], dma=B("w_in_b"))
        for c in range(16):
            S.op("pool", (lambda e, c=c: e.dma_start(out=xTb[:, c * 512:(c + 1) * 512], in_=xT[:, c * 512:(c + 1) * 512])),
                 writes=[B("xTb", c)], dma=B("xTb", c))
        for r in range(8):
            S.op("pool", (lambda e, r=r: e.dma_start(out=w_out_b[r * 128:(r + 1) * 128, :], in_=w_out[r * 128:(r + 1) * 128, :])),
                 writes=[B("w_out_b")], dma=B("w_out_b"))
        w_up_v = w_up.rearrange("(dc p) n -> p dc n", p=128)
        for j in range(NJ):
            for part in range(2):
                c0 = part * DFF + j * 128
                S.op("pool", (lambda e, j=j, part=part, c0=c0: e.dma_start(
                    out=w_up_b[j, :, :, part * 128:(part + 1) * 128], in_=w_up_v[:, :, c0:c0 + 128])),
                    writes=[B("w_up_b", j)], dma=B("w_up_b", j))
        for j in range(NJ):
            S.op("pool", (lambda e, j=j: e.dma_start(out=w_down_b[j * 128:(j + 1) * 128, :], in_=w_down[j * 128:(j + 1) * 128, :])),
                 writes=[B("w_down_b", j)], dma=B("w_down_b", j))

        S.op("sp", lambda e: e.dma_start(out=cstf[:], in_=cst[:, :]), writes=[B("cstf")], dma=B("cstf"))
        S.op("sp", lambda e: e.dma_start(out=flagt[:], in_=flag[0:1, :].partition_broadcast(128)), writes=[B("flagt")], dma=B("flagt"))
        S.op("dve", lambda e: e.tensor_copy(out=trib[:], in_=cstf[:, 0:128]), reads=[B("cstf")], writes=[B("trib")])
        S.op("dve", lambda e: e.tensor_copy(out=identb[:], in_=cstf[:, 128:256]), reads=[B("cstf")], writes=[B("identb")])
        S.op("dve", lambda e: e.memset(onesb[:], 1.0), writes=[B("onesb")])
        S.op("dve", lambda e: e.memset(onesf[:], 1.0), writes=[B("onesf")])

        w_in_v = w_in_b.rearrange("(dc p) n -> p dc n", p=128)
        xTb_v = xTb.rearrange("(dc p) t -> p dc t", p=128)

        def dbg_dump(key, tile_ap, bufl):
            if key in dbg_out:
                S.op("sp", lambda e: e.dma_start(out=dbg_out[key][:, :], in_=tile_ap), reads=bufl, dma=B("dbg", key))

        ab = ExitStack()
        with ab:
            KT = sb("KT", [128, SEQ], BF16, ab)
            V = sb("V", [128, NB * 128], BF16, ab)
            QT = sb("QT", [128, NQ], BF16, ab)
            xcs = [sb("xc%d" % i, [128, 8 * 512], BF16, ab) for i in range(2)]
            wq = sb("wq", [128, 8 * 128], BF16, ab)
            wk = sb("wk", [128, 8 * 128], BF16, ab)
            wv = sb("wv", [128, 8 * 128], BF16, ab)
            wf = sb("wf", [128, 8 * 8], BF16, ab)
            pTs = [sb("pT%d" % i, [128, 512], BF16, ab) for i in range(4)]
            rec = sb("rec", [128, 512], F32, ab)
            Fz = sb("Fz", [128, 512], F32, ab)
            Lt = sb("Lt", [128, 512], F32, ab)
            TOT = sb("TOT", [128, 512], F32, ab)
            offs = sb("offs", [128, 512], F32, ab)
            cumL = sb("cumL", [128, 512], F32, ab)
            bfgt = sb("bfgt", [128, 512], F32, ab)
            maskAt = sb("maskAt", [128, NG * 64], F32, ab)
            maskBt = sb("maskBt", [128, NG * 8], F32, ab)
            biasAs = [sb("biasA%d" % i, [128, 128], F32, ab) for i in range(2)]
            biasBb = sb("biasBb", [128, 8 * 5 * 128], BF16, ab)

            S.op("sp", lambda e: e.dma_start(out=bfgt[:], in_=bfg[0:1, :].partition_broadcast(128)), writes=[B("bfgt")], dma=B("bfgt"))
            S.op("sp", lambda e: e.dma_start(out=maskAt[:], in_=maskA[0:1, :].partition_broadcast(128)), writes=[B("maskAt")], dma=B("maskAt"))
            S.op("sp", lambda e: e.dma_start(out=maskBt[:], in_=maskB[0:1, :].partition_broadcast(128)), writes=[B("maskBt")], dma=B("maskBt"))
            S.op("pool", lambda e: e.dma_start(out=biasBb[:], in_=biasB[:, :]), writes=[B("biasBb")], dma=B("biasBb"))
            S.op("dve", lambda e: e.tensor_scalar(out=biasBb[:], in0=biasBb[:], scalar1=8.0, scalar2=None, op0=ALU.mult),
                 reads=[B("biasBb")], writes=[B("biasBb")])

            rot = {"S": 0, "P": 0, "proj": 0, "bA": 0}

            def proj_pass(kind, hp):
                if kind == "A":
                    qc, kc, vc = hp * 128, 512 + hp * 128, 1024 + hp * 128
                    c_start = 0
                else:
                    qc, kc, vc = 1544 + hp * 128, 2056 + hp * 128, 2568 + hp * 128
                    c_start = 6
                for (t, c0, nm) in ((wq, qc, "wq"), (wk, kc, "wk"), (wv, vc, "wv")):
                    S.op("sp", (lambda e, t=t, c0=c0: e.dma_start(out=t[:].rearrange("p (dc n) -> p dc n", dc=8),
                                                                  in_=w_in_v[:, :, c0:c0 + 128])),
                         reads=[B("w_in_b")], writes=[B(nm)], dma=B(nm))
                do_f = (kind == "A" and hp == 0)
                if do_f:
                    S.op("sp", lambda e: e.dma_start(out=wf[:].rearrange("p (dc n) -> p dc n", dc=8), in_=w_in_v[:, :, 1536:1544]),
                         reads=[B("w_in_b")], writes=[B("wf")], dma=B("wf"))
                for c in range(c_start, 16):
                    xc = xcs[c % 2]
                    Bxc = B("xc", c % 2)
                    S.op("sp", (lambda e, xc=xc, c=c: e.dma_start(out=xc[:].rearrange("p (dc t) -> p dc t", dc=8),
                                                                  in_=xTb_v[:, :, c * 512:(c + 1) * 512])),
                         reads=[B("xTb", c)], writes=[Bxc], dma=Bxc)
                    bi = 5 + rot["proj"] % 2
                    rot["proj"] += 1
                    ps = banks[bi]
                    for dc in range(8):
                        S.op("pe", (lambda e, ps=ps, xc=xc, dc=dc: e.matmul(ps[:, 0:512], lhsT=wk[:, dc * 128:(dc + 1) * 128],
                                                                         rhs=xc[:, dc * 512:(dc + 1) * 512], start=(dc == 0), stop=(dc == 7))),
                             reads=[B("wk"), Bxc], writes=[Bbank[bi]])
                    S.op("act", (lambda e, ps=ps, c=c: e.copy(out=KT[:, c * 512:(c + 1) * 512], in_=ps[:, 0:512])),
                         reads=[Bbank[bi]], writes=[B("KT", c)])
                    if c >= 7:
                        if c == 7:
                            xs, n, qo = 384, 128, 0
                        else:
                            xs, n, qo = 0, 512, 128 + (c - 8) * 512
                        bi = 5 + rot["proj"] % 2
                        rot["proj"] += 1
                        ps = banks[bi]
                        for dc in range(8):
                            S.op("pe", (lambda e, ps=ps, xc=xc, dc=dc, xs=xs, n=n: e.matmul(
                                ps[:, 0:n], lhsT=wq[:, dc * 128:(dc + 1) * 128], rhs=xc[:, dc * 512 + xs:dc * 512 + xs + n],
                                start=(dc == 0), stop=(dc == 7))), reads=[B("wq"), Bxc], writes=[Bbank[bi]])
                        S.op("act", (lambda e, ps=ps, n=n, qo=qo: e.copy(out=QT[:, qo:qo + n], in_=ps[:, 0:n])),
                             reads=[Bbank[bi]], writes=[B("QT", qo)])
                    bi = 5 + rot["proj"] % 2
                    rot["proj"] += 1
                    ps = banks[bi]
                    for blk in range(4):
                        for dc in range(8):
                            S.op("pe", (lambda e, ps=ps, xc=xc, dc=dc, blk=blk: e.matmul(
                                ps[:, blk * 128:(blk + 1) * 128], lhsT=xc[:, dc * 512 + blk * 128:dc * 512 + (blk + 1) * 128],
                                rhs=wv[:, dc * 128:(dc + 1) * 128], start=(dc == 0), stop=(dc == 7))),
                                reads=[B("wv"), Bxc], writes=[Bbank[bi]])
                    S.op("dve", (lambda e, ps=ps, c=c: e.tensor_copy(out=V[:, c * 512:(c + 1) * 512], in_=ps[:, 0:512])),
                         reads=[Bbank[bi]], writes=[B("V", c)])
                    if do_f:
                        bi = 5 + rot["proj"] % 2
                        rot["proj"] += 1
                        ps = banks[bi]
                        for blk in range(4):
                            for dc in range(8):
                                S.op("pe", (lambda e, ps=ps, xc=xc, dc=dc, blk=blk: e.matmul(
                                    ps[:, blk * 8:(blk + 1) * 8], lhsT=xc[:, dc * 512 + blk * 128:dc * 512 + (blk + 1) * 128],
                                    rhs=wf[:, dc * 8:(dc + 1) * 8], start=(dc == 0), stop=(dc == 7))),
                                    reads=[B("wf"), Bxc], writes=[Bbank[bi]])
                        S.op("dve", (lambda e, ps=ps, c=c: e.tensor_copy(out=Fz[:, c * 32:(c + 1) * 32], in_=ps[:, 0:32])),
                             reads=[Bbank[bi]], writes=[B("Fz")])

            def cum_compute():
                S.op("dve", lambda e: e.tensor_tensor(out=Fz[:], in0=Fz[:], in1=bfgt[:], op=ALU.add), reads=[B("Fz"), B("bfgt")], writes=[B("Fz")])
                S.op("dve", lambda e: e.tensor_scalar_max(out=Fz[:], in0=Fz[:], scalar1=-30.0), reads=[B("Fz")], writes=[B("Fz")])
                S.op("act", lambda e: e.activation(out=Lt[:], in_=Fz[:], func=AF.Exp, scale=-1.0), reads=[B("Fz")], writes=[B("Lt")])
                S.op("act", lambda e: e.activation(out=Lt[:], in_=Lt[:], func=AF.Ln, bias=1.0, scale=1.0), reads=[B("Lt")], writes=[B("Lt")])
                S.op("pe", lambda e: e.matmul(banks[5][:, 0:512], lhsT=utri, rhs=Lt[:], start=True, stop=True),
                     reads=[B("Lt"), B("cstf")], writes=[Bbank[5]])
                S.op("pe", lambda e: e.matmul(banks[6][:, 0:512], lhsT=onesf[:], rhs=Lt[:], start=True, stop=True),
                     reads=[B("Lt"), B("onesf")], writes=[Bbank[6]])
                S.op("dve", lambda e: e.tensor_copy(out=TOT[:], in_=banks[6][:, 0:512]), reads=[Bbank[6]], writes=[B("TOT")])
                S.op("dve", lambda e: e.memset(offs[:, 0:8], 0.0), writes=[B("offs")])
                for kb in range(1, NB):
                    S.op("dve", (lambda e, kb=kb: e.tensor_tensor(out=offs[:, kb * 8:(kb + 1) * 8], in0=offs[:, (kb - 1) * 8:kb * 8],
                                                                  in1=TOT[:, (kb - 1) * 8:kb * 8], op=ALU.add)),
                         reads=[B("offs"), B("TOT")], writes=[B("offs")])
                S.op("dve", lambda e: e.tensor_tensor(out=cumL[:], in0=banks[5][:, 0:512], in1=offs[:], op=ALU.add),
                     reads=[Bbank[5], B("offs")], writes=[B("cumL")])

            cumL_v = cumL[:].rearrange("p (kb h) -> p h kb", h=8)

            def attn_pass(kind, hp):
                oc = hp if kind == "A" else 4 + hp
                for g in range(NG):
                    kbs, N, qoff = ginfo(g)
                    nq = N // 128
                    if kind == "A":
                        nsteps = kbs + nq
                        kref = kbs + 2 if nq == 4 else kbs
                        bA = biasAs[rot["bA"] % 2]
                        BbA = B("biasA", rot["bA"] % 2)
                        rot["bA"] += 1
                        for h in range(2):
                            H = 2 * hp + h
                            S.op("dve", (lambda e, bA=bA, h=h, H=H, nsteps=nsteps, kref=kref, g=g: e.scalar_tensor_tensor(
                                out=bA[:, h * 64:h * 64 + nsteps], in0=cumL_v[:, H, 0:nsteps],
                                scalar=offs[:, kref * 8 + H:kref * 8 + H + 1], in1=maskAt[:, g * 64:g * 64 + nsteps],
                                op0=ALU.subtract, op1=ALU.add)),
                                reads=[B("cumL"), B("offs"), B("maskAt")], writes=[BbA])
                    else:
                        nsteps = nq + 4
                    po, pd = banks[3], banks[4]
                    for st in range(nsteps):
                        for h in range(2):
                            H = 2 * hp + h
                            r0, r1 = h * 64, (h + 1) * 64
                            si = rot["S"] % 3
                            rot["S"] += 1
                            ps = banks[si]
                            if kind == "A":
                                kb = st
                                i_lo, i_hi = max(0, kb - kbs), nq - 1
                            else:
                                kb = kbs - 4 + st
                                i_lo, i_hi = max(0, st - 4), min(nq - 1, st)
                            q_lo, q_hi = i_lo * 128, (i_hi + 1) * 128
                            Bk = B("KT", kb // 4)
                            Bv = B("V", kb // 4)
                            extra = []
                            if kind == "A":
                                if kb >= kbs:
                                    extra.append((i_lo, trib[:]))
                            else:
                                for i in range(i_lo, i_hi + 1):
                                    o = i - (st - 4)
                                    extra.append((i, biasBb[:, (H * 5 + o) * 128:(H * 5 + o + 1) * 128]))
                            S.op("pe", (lambda e, ps=ps, r0=r0, r1=r1, kb=kb, q_lo=q_lo, q_hi=q_hi, qoff=qoff, last=(not extra): e.matmul(
                                ps[:, q_lo:q_hi], lhsT=KT[r0:r1, kb * 128:(kb + 1) * 128], rhs=QT[r0:r1, qoff + q_lo:qoff + q_hi],
                                start=True, stop=last)), reads=[Bk, B("QT", qoff)], writes=[Bbank[si]])
                            for n_, (i, bt) in enumerate(extra):
                                S.op("pe", (lambda e, ps=ps, i=i, bt=bt, last=(n_ == len(extra) - 1): e.matmul(
                                    ps[:, i * 128:(i + 1) * 128], lhsT=identb[:], rhs=bt, start=False, stop=last)),
                                    reads=[B("identb"), B("trib"), B("biasBb")], writes=[Bbank[si]])
                            pi = rot["P"] % 4
                            rot["P"] += 1
                            pT = pTs[pi]
                            if kind == "A":
                                bias_ap = bA[:, h * 64 + kb:h * 64 + kb + 1]
                                rd = [Bbank[si], BbA]
                            else:
                                bias_ap = maskBt[:, g * 8 + st:g * 8 + st + 1]
                                rd = [Bbank[si], B("maskBt")]
                            S.op("act", (lambda e, pT=pT, ps=ps, q_lo=q_lo, q_hi=q_hi, bias_ap=bias_ap: e.activation(
                                out=pT[:, q_lo:q_hi], in_=ps[:, q_lo:q_hi], func=AF.Exp, bias=bias_ap, scale=0.125)),
                                reads=rd, writes=[B("pT", pi)])
                            first = (st == 0)
                            final = (st == nsteps - 1)
                            c0, c1 = q_lo, q_hi
                            S.op("pe", (lambda e, r0=r0, r1=r1, kb=kb, h=h, pT=pT, c0=c0, c1=c1, first=first, final=final: e.matmul(
                                po[r0:r1, c0:c1], lhsT=V[:, kb * 128 + h * 64:kb * 128 + (h + 1) * 64], rhs=pT[:, c0:c1],
                                start=first, stop=final, skip_group_check=True)), reads=[Bv, B("pT", pi)], writes=[Bbank[3]])
                            S.op("pe", (lambda e, r0=r0, r1=r1, pT=pT, c0=c0, c1=c1, first=first, final=final: e.matmul(
                                pd[r0:r1, c0:c1], lhsT=onesb[:, 0:64], rhs=pT[:, c0:c1],
                                start=first, stop=final, skip_group_check=True)), reads=[B("onesb"), B("pT", pi)], writes=[Bbank[4]])
                    S.op("dve", (lambda e, N=N: e.reciprocal(out=rec[:, 0:N], in_=pd[:, 0:N])), reads=[Bbank[4]], writes=[B("rec")])
                    S.op("dve", (lambda e, N=N, oc=oc, qoff=qoff: e.tensor_tensor(out=OT[oc][:, qoff:qoff + N], in0=po[:, 0:N],
                                                                                   in1=rec[:, 0:N], op=ALU.mult)),
                         reads=[Bbank[3], B("rec")], writes=[B("OT", oc, g)])

            for hp in range(STAGE["hpA"]):
                proj_pass("A", hp)
                if hp == 0:
                    cum_compute()
                    dbg_dump("cumL", cumL[:], [B("cumL")])
                attn_pass("A", hp)
            for hp in range(STAGE["hpB"]):
                proj_pass("B", hp)
                attn_pass("B", hp)
            for c in range(8):
                dbg_dump("OT%d" % c, OT[c][:], [B("OT", c, g) for g in range(NG)])
            S.flush(nc, es)

        pc = ExitStack()
        if not STAGE["C"]:
            S.op("sp", None, reads=[])
            return nc
        with pc:
            woutb = sb("woutb", [128, 8 * 1024], BF16, pc)
            lnt = sb("lnt", [128, 4 * 1024], F32, pc)
            cpt = sb("cpt", [128, 44 * 4], F32, pc)
            tails = sb("tails", [128, 44 * 2], F32, pc)
            x1g = sb("x1g", [128, 4 * 1024], F32, pc)
            xblk = [sb("xblk%d" % i, [128, 1024], F32, pc) for i in range(2)]
            ybuf = sb("ybuf", [128, 1024], F32, pc)
            obuf = [sb("obuf%d" % i, [128, 1024], F32, pc) for i in range(2)]
            x1b = sb("x1b", [128, 1024], BF16, pc)
            x1T = sb("x1T", [128, 8 * 512], BF16, pc)
            hT = sb("hT", [128, NJ * 512], BF16, pc)
            Uv = [sb("Uv%d" % i, [128, 514], F32, pc) for i in range(2)]
            Ug = [sb("Ug%d" % i, [128, 514], F32, pc) for i in range(2)]
            av = [sb("av%d" % i, [128, 512], F32, pc) for i in range(2)]
            ag = [sb("ag%d" % i, [128, 512], F32, pc) for i in range(2)]
            wus = [sb("wu%d" % i, [128, 8 * 256], BF16, pc) for i in range(3)]
            wds = [sb("wd%d" % i, [128, 1024], BF16, pc) for i in range(2)]
            st = sb("st", [128, 16], F32, pc)
            ptmp = sb("ptmp", [128, 512], F32, pc)

            S.op("sp", lambda e: e.dma_start(out=woutb[:].rearrange("p (c n) -> p c n", c=8),
                                             in_=w_out_b.rearrange("(c p) n -> p c n", p=128)),
                 reads=[B("w_out_b")], writes=[B("woutb")], dma=B("woutb"))
            for i in range(4):
                S.op("sp", (lambda e, i=i: e.dma_start(out=lnt[:, i * 1024:(i + 1) * 1024], in_=lnp[i:i + 1, :].partition_broadcast(128))),
                     writes=[B("lnt")], dma=B("lnt"))
            S.op("sp", lambda e: e.dma_start(out=cpt[:], in_=convp[:, :]), writes=[B("cpt")], dma=B("cpt"))

            def layer_norm(src, Bsrc, dst, Bdst, gi):
                S.op("dve", lambda e: e.memset(st[:, 0:2], 0.0), writes=[B("st")])
                S.op("act", lambda e: e.activation(out=dst, in_=src, func=AF.Identity, accum_out=st[:, 0:1]),
                     reads=[Bsrc, B("st")], writes=[Bdst, B("st")])
                S.op("act", lambda e: e.activation(out=dst, in_=src, func=AF.Square, accum_out=st[:, 1:2]),
                     reads=[Bsrc, B("st")], writes=[Bdst, B("st")])
                S.op("dve", lambda e: e.tensor_scalar(out=st[:, 2:3], in0=st[:, 0:1], scalar1=1.0 / D, scalar2=None, op0=ALU.mult),
                     reads=[B("st")], writes=[B("st")])
                S.op("dve", lambda e: e.tensor_tensor(out=st[:, 3:4], in0=st[:, 2:3], in1=st[:, 2:3], op=ALU.mult),
                     reads=[B("st")], writes=[B("st")])
                S.op("dve", lambda e: e.scalar_tensor_tensor(out=st[:, 4:5], in0=st[:, 1:2], scalar=1.0 / D, in1=st[:, 3:4],
                                                             op0=ALU.mult, op1=ALU.subtract), reads=[B("st")], writes=[B("st")])
                S.op("dve", lambda e: e.tensor_scalar(out=st[:, 5:6], in0=st[:, 4:5], scalar1=EPS, scalar2=None, op0=ALU.add),
                     reads=[B("st")], writes=[B("st")])
                S.op("act", lambda e: e.activation(out=st[:, 6:7], in_=st[:, 5:6], func=AF.Sqrt), reads=[B("st")], writes=[B("st")])
                S.op("dve", lambda e: e.reciprocal(out=st[:, 7:8], in_=st[:, 6:7]), reads=[B("st")], writes=[B("st")])
                S.op("dve", lambda e: e.tensor_scalar(out=dst, in0=src, scalar1=st[:, 2:3], scalar2=st[:, 7:8],
                                                      op0=ALU.subtract, op1=ALU.mult), reads=[Bsrc, B("st")], writes=[Bdst])
                S.op("pool", lambda e: e.tensor_tensor(out=dst, in0=dst, in1=lnt[:, gi * 1024:(gi + 1) * 1024], op=ALU.mult),
                     reads=[Bdst, B("lnt")], writes=[Bdst])
                S.op("pool", lambda e: e.tensor_tensor(out=dst, in0=dst, in1=lnt[:, (gi + 1) * 1024:(gi + 2) * 1024], op=ALU.add),
                     reads=[Bdst, B("lnt")], writes=[Bdst])

            rc = {"x": 0, "wu": 0, "wd": 0, "u": 0, "o": 0, "up": 0}
            for g in range(NG):
                kbs, N, qoff = ginfo(g)
                nq = N // 128
                for tb in range(nq):
                    t0 = qoff + tb * 128
                    xi = rc["x"] % 2
                    rc["x"] += 1
                    xb = xblk[xi]
                    S.op("sp", (lambda e, xb=xb, t0=t0: e.dma_start(out=xb[:], in_=xo[t0:t0 + 128, :])), writes=[B("xblk", xi)], dma=B("xblk", xi))
                    for half in range(2):
                        ps = banks[half]
                        for c in range(8):
                            S.op("pe", (lambda e, ps=ps, c=c, t0=t0, half=half: e.matmul(
                                ps[:, 0:512], lhsT=OT[c][:, t0:t0 + 128], rhs=woutb[:, c * 1024 + half * 512:c * 1024 + (half + 1) * 512],
                                start=(c == 0), stop=(c == 7))), reads=[B("OT", c, g), B("woutb")], writes=[Bbank[half]])
                        S.op("dve", (lambda e, ps=ps, xb=xb, half=half: e.scalar_tensor_tensor(
                            out=ybuf[:, half * 512:(half + 1) * 512], in0=xb[:, half * 512:(half + 1) * 512], scalar=ALPHA,
                            in1=ps[:, 0:512], op0=ALU.mult, op1=ALU.add)), reads=[Bbank[half], B("xblk", xi)], writes=[B("ybuf")])
                    x1 = x1g[:, tb * 1024:(tb + 1) * 1024]
                    layer_norm(ybuf[:], B("ybuf"), x1, B("x1g", tb), 0)
                    S.op("pool", (lambda e, x1=x1: e.tensor_copy(out=x1b[:], in_=x1)), reads=[B("x1g", tb)], writes=[B("x1b")])
                    for dc in range(8):
                        S.op("pe", (lambda e, dc=dc: e.transpose(bankT[:, dc * 128:(dc + 1) * 128], x1b[:, dc * 128:(dc + 1) * 128], identb[:])),
                             reads=[B("x1b"), B("identb")], writes=[BbankT])
                    S.op("act", (lambda e, N=N, tb=tb: e.copy(
                        out=x1T[:, 0:8 * N].rearrange("p (dc t) -> p dc t", dc=8)[:, :, tb * 128:(tb + 1) * 128],
                        in_=bankT[:, 0:1024].rearrange("p (dc t) -> p dc t", dc=8))), reads=[BbankT], writes=[B("x1T")])
                for j in range(NJ):
                    wi = rc["wu"] % 3
                    rc["wu"] += 1
                    wu = wus[wi]
                    S.op("sp", (lambda e, wu=wu, j=j: e.dma_start(out=wu[:].rearrange("p (dc n) -> p dc n", dc=8), in_=w_up_b[j])),
                         reads=[B("w_up_b", j)], writes=[B("wu", wi)], dma=B("wu", wi))
                    ui = rc["up"] % 2
                    rc["up"] += 1
                    bv, bg = 3 + 2 * ui, 4 + 2 * ui
                    for (bi, off) in ((bv, 0), (bg, 128)):
                        ps = banks[bi]
                        for dc in range(8):
                            S.op("pe", (lambda e, ps=ps, wu=wu, dc=dc, off=off, N=N: e.matmul(
                                ps[:, 0:N], lhsT=wu[:, dc * 256 + off:dc * 256 + off + 128], rhs=x1T[:, dc * N:(dc + 1) * N],
                                start=(dc == 0), stop=(dc == 7))), reads=[B("wu", wi), B("x1T")], writes=[Bbank[bi]])
                    if g == 0:
                        S.op("dve", (lambda e, j=j, bv=bv, N=N: e.tensor_scalar(out=tails[:, j * 2:j * 2 + 2], in0=banks[bv][:, N - 2:N],
                                                                                 scalar1=flagt[:, 0:1], scalar2=None, op0=ALU.mult)),
                             reads=[Bbank[bv], B("flagt")], writes=[B("tails", j)])
                        S.op("dve", (lambda e, j=j, bg=bg, N=N: e.tensor_scalar(out=tails[:, (NJ + j) * 2:(NJ + j) * 2 + 2], in0=banks[bg][:, N - 2:N],
                                                                                 scalar1=flagt[:, 0:1], scalar2=None, op0=ALU.mult)),
                             reads=[Bbank[bg], B("flagt")], writes=[B("tails", NJ + j)])
                        continue
                    k = rc["u"] % 2
                    rc["u"] += 1
                    for (U, Bn, bi, jj, acc, accn, eng) in ((Uv[k], "Uv", bv, j, av[k], "av", "dve"), (Ug[k], "Ug", bg, NJ + j, ag[k], "ag", "pool")):
                        BU = B(Bn, k)
                        Bacc = B(accn, k)
                        S.op("act", (lambda e, U=U, bi=bi: e.copy(out=U[:, 2:514], in_=banks[bi][:, 0:512])), reads=[Bbank[bi]], writes=[BU])
                        S.op("pool", (lambda e, U=U, jj=jj: e.tensor_copy(out=U[:, 0:2], in_=tails[:, jj * 2:jj * 2 + 2])),
                             reads=[B("tails", jj)], writes=[BU])
                        S.op(eng, (lambda e, U=U, acc=acc, jj=jj: e.tensor_scalar(
                            out=acc[:], in0=U[:, 2:514], scalar1=cpt[:, jj * 4 + 2:jj * 4 + 3], scalar2=cpt[:, jj * 4 + 3:jj * 4 + 4],
                            op0=ALU.mult, op1=ALU.add)), reads=[BU, B("cpt")], writes=[Bacc])
                        for (lo_, wi_) in ((1, 1), (0, 0)):
                            if eng == "dve":
                                S.op(eng, (lambda e, U=U, acc=acc, jj=jj, lo_=lo_, wi_=wi_: e.scalar_tensor_tensor(
                                    out=acc[:], in0=U[:, lo_:lo_ + 512], scalar=cpt[:, jj * 4 + wi_:jj * 4 + wi_ + 1], in1=acc[:],
                                    op0=ALU.mult, op1=ALU.add)), reads=[BU, B("cpt"), Bacc], writes=[Bacc])
                            else:
                                S.op(eng, (lambda e, U=U, jj=jj, lo_=lo_, wi_=wi_: e.tensor_scalar(
                                    out=ptmp[:], in0=U[:, lo_:lo_ + 512], scalar1=cpt[:, jj * 4 + wi_:jj * 4 + wi_ + 1], scalar2=None,
                                    op0=ALU.mult)), reads=[BU, B("cpt")], writes=[B("ptmp")])
                                S.op(eng, (lambda e, acc=acc: e.tensor_tensor(out=acc[:], in0=acc[:], in1=ptmp[:], op=ALU.add)),
                                     reads=[Bacc, B("ptmp")], writes=[Bacc])
                        S.op("pool", (lambda e, U=U, jj=jj: e.tensor_copy(out=tails[:, jj * 2:jj * 2 + 2], in_=U[:, 512:514])),
                             reads=[BU], writes=[B("tails", jj)])
                    S.op("act", (lambda e, k=k: e.activation(out=ag[k][:], in_=ag[k][:], func=AF.Gelu_apprx_tanh)),
                         reads=[B("ag", k)], writes=[B("ag", k)])
                    S.op("dve", (lambda e, k=k, j=j: e.tensor_tensor(out=hT[:, j * 512:(j + 1) * 512], in0=av[k][:], in1=ag[k][:], op=ALU.mult)),
                         reads=[B("av", k), B("ag", k)], writes=[B("hT", j)])
                if g == 0:
                    continue
                for hh in range(2):
                    for j in range(NJ):
                        wi = rc["wd"] % 2
                        rc["wd"] += 1
                        wd = wds[wi]
                        S.op("sp", (lambda e, wd=wd, j=j: e.dma_start(out=wd[:], in_=w_down_b[j * 128:(j + 1) * 128, :])),
                             reads=[B("w_down_b", j)], writes=[B("wd", wi)], dma=B("wd", wi))
                        for tbl in range(2):
                            tb = 2 * hh + tbl
                            for half in range(2):
                                bi = 3 + tbl * 2 + half
                                S.op("pe", (lambda e, bi=bi, wd=wd, j=j, tb=tb, half=half: e.matmul(
                                    banks[bi][:, 0:512], lhsT=hT[:, j * 512 + tb * 128:j * 512 + (tb + 1) * 128],
                                    rhs=wd[:, half * 512:(half + 1) * 512], start=(j == 0), stop=(j == NJ - 1))),
                                    reads=[B("hT", j), B("wd", wi)], writes=[Bbank[bi]])
                    for tbl in range(2):
                        tb = 2 * hh + tbl
                        for half in range(2):
                            bi = 3 + tbl * 2 + half
                            S.op("dve", (lambda e, bi=bi, tb=tb, half=half: e.scalar_tensor_tensor(
                                out=ybuf[:, half * 512:(half + 1) * 512], in0=x1g[:, tb * 1024 + half * 512:tb * 1024 + (half + 1) * 512],
                                scalar=ALPHA, in1=banks[bi][:, 0:512], op0=ALU.mult, op1=ALU.add)),
                                reads=[Bbank[bi], B("x1g", tb)], writes=[B("ybuf")])
                        oi = rc["o"] % 2
                        rc["o"] += 1
                        ob = obuf[oi]
                        layer_norm(ybuf[:], B("ybuf"), ob[:], B("obuf", oi), 2)
                        r0 = (g - 1) * 512 + tb * 128
                        S.op("pool", (lambda e, ob=ob, r0=r0: e.dma_start(out=out[r0:r0 + 128, :], in_=ob[:])),
                             reads=[B("obuf", oi)], dma=B("obuf_st", oi))
                        B("obuf", oi).r[("d", "st")] = ("d", B("obuf_st", oi), B("obuf_st", oi).cnt)
            fin = [B("obuf_st", 0), B("obuf_st", 1)] + [B("dbg", k) for k in dbg_out]
            for b in fin:
                if b.cnt:
                    b.w = ("d", b, b.cnt)
            S.op("sp", None, reads=[b for b in fin if b.cnt])
            S.flush(nc, es)
    return nc


def host_inputs(x, w_in, b_forget, rel_bias, w_out, ln1_g, ln1_b, w_up, conv_w, conv_b, w_down, ln2_g, ln2_b):
    f = np.float32
    x = np.asarray(x, f)
    w_in0, w_out0, w_up0, w_down0 = (np.ascontiguousarray(np.asarray(a, f)[0]) for a in (w_in, w_out, w_up, w_down))
    lnp = np.ascontiguousarray(np.stack([np.asarray(a, f)[0] for a in (ln1_g, ln1_b, ln2_g, ln2_b)]))
    cw = np.asarray(conv_w, f)[0]
    cb = np.asarray(conv_b, f)[0]
    convp = np.zeros((128, 44, 4), f)
    for j in range(44):
        base = j * 128 if j < NJ else DFF + (j - NJ) * 128
        convp[:, j, 0:3] = cw[:, base:base + 128].T
        convp[:, j, 3] = cb[base:base + 128]
    convp = np.ascontiguousarray(convp.reshape(128, 176))
    bfgv = np.ascontiguousarray(np.tile(np.asarray(b_forget, f)[0], NB)[None, :])
    rb = np.asarray(rel_bias, f)[0]
    kk = np.arange(128)[:, None]
    qq = np.arange(128)[None, :]
    tiles = np.zeros((128, 8, 5, 128), f)
    for o in range(5):
        idx = np.clip(qq - kk + 128 * o, -128, 128) + 128
        t = rb[:, idx]
        t = np.transpose(t, (1, 0, 2)).copy()
        if o == 0:
            t[:, :, :][np.broadcast_to(((kk >= 64) & (qq < 64))[:, None, :], t.shape)] = NEG
        if o == 4:
            t[np.broadcast_to(((kk < 64) & (qq >= 64))[:, None, :], t.shape)] = NEG
        tiles[:, :, o, :] = t
    biasB = np.ascontiguousarray(tiles.reshape(128, 8 * 5 * 128))
    tri = np.where(kk <= qq, 0.0, NEG).astype(f)
    ident = np.eye(128, dtype=f)
    utri = (kk <= qq).astype(f)
    cst = np.ascontiguousarray(np.concatenate([tri, ident, utri], axis=1))
    maps = []
    for core in range(8):
        b, hi = core // 2, core % 2
        own = x[b, hi * HALF:(hi + 1) * HALF]
        prev = x[b, 0:HALF] if hi else own
        store = np.concatenate([prev, own], axis=0)
        xTm = np.ascontiguousarray(store.T)
        xom = np.ascontiguousarray(store[31 * 128:])
        mA = np.zeros((NG, 64), f)
        mB = np.zeros((NG, 8), f)
        if not hi:
            for g in range(NG):
                kbs, N, _ = ginfo(g)
                lim = 31 if g == 0 else 32
                mA[g, :lim] = NEG
                for st in range(8):
                    if kbs - 4 + st < lim:
                        mB[g, st] = NEG
        maps.append({"xT": xTm, "xo": xom, "w_in": w_in0, "w_out": w_out0, "w_up": w_up0, "w_down": w_down0,
                     "lnp": lnp, "convp": convp, "bfg": bfgv, "biasB": biasB,
                     "maskA": np.ascontiguousarray(mA.reshape(1, -1)), "maskB": np.ascontiguousarray(mB.reshape(1, -1)),
                     "flag": np.full((1, 1), float(hi), f), "cst": cst})
    return maps


_NC = None


def kernel(**inputs):
    global _NC
    maps = host_inputs(**inputs)
    if _NC is None:
        _NC = build_nc()
    res = run_bass_kernel_spmd(_NC, maps, core_ids=list(range(8)))
    outp = np.empty((4, SEQ, D), np.float32)
    for core in range(8):
        b, hi = core // 2, core % 2
        outp[b, hi * HALF:(hi + 1) * HALF] = np.asarray(res.results[core]["out"], np.float32)
    kernel.last_results = res
    return outp
```

```python
import numpy as np
from contextlib import ExitStack
import concourse.bass as bass
import concourse.mybir as mybir
from concourse.bass_utils import run_bass_kernel_spmd

F32, BF16 = mybir.dt.float32, mybir.dt.bfloat16
AF = mybir.ActivationFunctionType
ALU = mybir.AluOpType

D = 1024
SEQ = 8192
HALF = 4096
NB = 64
NG = 9
NQ = 4224
DFF = 2816
NJ = 22
PC = 3080
ALPHA = float(2.0 ** 0.25)
EPS = 1e-5
NEG = -30000.0
DEBUG = {}
STAGE = {"hpA": 4, "hpB": 4, "C": True}


def ginfo(g):
    if g == 0:
        return 31, 128, 0
    return 32 + 4 * (g - 1), 512, 128 + 512 * (g - 1)


class Buf:
    __slots__ = ("name", "w", "r", "sem", "cnt")

    def __init__(self, name):
        self.name = name
        self.w = None
        self.r = {}
        self.sem = None
        self.cnt = 0


class Op:
    __slots__ = ("fn", "waits", "signal", "dma")

    def __init__(self, fn, waits, dma):
        self.fn = fn
        self.waits = waits
        self.signal = False
        self.dma = dma


class Sched:
    ENG = ("pe", "act", "dve", "pool", "sp")

    def __init__(self):
        self.ops = {e: [] for e in self.ENG}
        self.bufs = {}
        self.dmabufs = []

    def B(self, *key):
        b = self.bufs.get(key)
        if b is None:
            b = Buf(key)
            self.bufs[key] = b
        return b

    def op(self, eng, fn, reads=(), writes=(), dma=None):
        deps = set()
        for b in reads:
            if b.w is not None:
                deps.add(b.w)
        for b in writes:
            if b.w is not None:
                deps.add(b.w)
            deps.update(b.r.values())
        seq = len(self.ops[eng])
        if dma is not None:
            if dma.cnt == 0:
                self.dmabufs.append(dma)
            dma.cnt += 16
            ev = ("d", dma, dma.cnt)
            rkey = ("d", id(dma))
        else:
            ev = ("e", eng, seq)
            rkey = eng
        waits = []
        for d in deps:
            if d[0] == "e" and d[1] == eng and (eng == "pe" or dma is not None and False):
                continue
            if d[0] == "e":
                fr = getattr(self, "frozen", None)
                if fr is not None and d[2] <= fr[d[1]]:
                    if not (self.ops[d[1]][d[2]].signal and self.ops[d[1]][d[2]].fn is not None):
                        continue
                else:
                    self.ops[d[1]][d[2]].signal = True
            waits.append(d)
        for b in reads:
            b.r[rkey] = ev
        for b in writes:
            b.w = ev
            b.r = {}
        self.ops[eng].append(Op(fn, waits, dma))
        return ev

    def flush(self, nc, es):
        if not hasattr(self, "sems"):
            self.sems = {e: es.enter_context(nc.semaphore("sem_" + e)) for e in self.ENG}
            self.done = {e: 0 for e in self.ENG}
            self.frozen = {e: -1 for e in self.ENG}
            self.nds = 0
        evs = []
        for e in self.ENG:
            n = len(self.ops[e])
            if n:
                j = n - 1
                while j >= 0 and (self.ops[e][j].fn is None or self.ops[e][j].dma is not None):
                    j -= 1
                if j >= 0:
                    self.ops[e][j].signal = True
                    evs.append(("e", e, j))
        for b in self.dmabufs:
            evs.append(("d", b, b.cnt))
        for e in self.ENG:
            self.ops[e].append(Op(None, list(evs), None))
        for b in self.dmabufs:
            if b.sem is None:
                b.sem = es.enter_context(nc.semaphore("dsem%d" % self.nds))
                self.nds += 1
        cum = {}
        for e in self.ENG:
            c = 0
            arr = []
            for o in self.ops[e]:
                if o.signal and o.dma is None and o.fn is not None:
                    c += 1
                arr.append(c)
            cum[e] = arr
        sems = self.sems
        sched = self
        lo = dict(self.done)
        if not hasattr(self, "waited"):
            self.waited = {e: {} for e in self.ENG}

        def replay(e, eobj):
            waited = sched.waited[e]
            for o in sched.ops[e][lo[e]:]:
                for d in o.waits:
                    if d[0] == "e":
                        sem, val = sems[d[1]], cum[d[1]][d[2]]
                    else:
                        sem, val = d[1].sem, d[2]
                    k = id(sem)
                    if waited.get(k, 0) >= val:
                        continue
                    waited[k] = val
                    eobj.wait_ge(sem, val)
                if o.fn is None:
                    continue
                ins = o.fn(eobj)
                if o.dma is not None:
                    ins.then_inc(o.dma.sem, 16)
                elif o.signal:
                    ins.then_inc(sems[e], 1)

        with nc.Block() as block:
            @block.tensor
            def _(t):
                replay("pe", t)

            @block.scalar
            def _(t):
                replay("act", t)

            @block.vector
            def _(t):
                replay("dve", t)

            @block.gpsimd
            def _(t):
                replay("pool", t)

            @block.sync
            def _(t):
                replay("sp", t)
        for e in self.ENG:
            self.done[e] = len(self.ops[e])
            self.frozen[e] = len(self.ops[e]) - 1


def build_nc():
    nc = bass.Bass("TRN2", target_bir_lowering=False)
    S = Sched()
    B = S.B

    def din(name, shape):
        return nc.dram_tensor(name, shape, F32, kind="ExternalInput").ap()

    xT = din("xT", [D, SEQ])
    xo = din("xo", [NQ, D])
    w_in = din("w_in", [D, PC])
    w_out = din("w_out", [D, D])
    w_up = din("w_up", [D, 2 * DFF])
    w_down = din("w_down", [DFF, D])
    lnp = din("lnp", [4, D])
    convp = din("convp", [128, 44 * 4])
    bfg = din("bfg", [1, 512])
    biasB = din("biasB", [128, 8 * 5 * 128])
    maskA = din("maskA", [1, NG * 64])
    maskB = din("maskB", [1, NG * 8])
    flag = din("flag", [1, 1])
    cst = din("cst", [128, 3 * 128])
    out = nc.dram_tensor("out", [HALF, D], F32, kind="ExternalOutput").ap()

    xTb = nc.dram_tensor("xTb", [D, SEQ], BF16).ap()
    w_in_b = nc.dram_tensor("w_in_b", [D, PC], BF16).ap()
    w_out_b = nc.dram_tensor("w_out_b", [D, D], BF16).ap()
    w_up_b = nc.dram_tensor("w_up_b", [NJ, 128, 8, 256], BF16).ap()
    w_down_b = nc.dram_tensor("w_down_b", [DFF, D], BF16).ap()

    dbg_out = {}
    for k, shp in DEBUG.items():
        dbg_out[k] = nc.dram_tensor("dbg_" + k, list(shp[0]), shp[1], kind="ExternalOutput").ap()

    es = ExitStack()
    with es:
        def sb(name, shape, dt, stack=es):
            return stack.enter_context(nc.sbuf_tensor(name, shape, dt))

        Bbank = [B("bank", i) for i in range(8)]
        BbankT = B("bankT")

        OT = [sb("OT%d" % c, [128, NQ], BF16) for c in range(8)]
        cstf = sb("cstf", [128, 384], F32)
        trib = sb("trib", [128, 128], BF16)
        identb = sb("identb", [128, 128], BF16)
        onesb = sb("onesb", [128, 128], BF16)
        onesf = sb("onesf", [128, 128], F32)
        flagt = sb("flagt", [128, 1], F32)
        utri = cstf[:, 256:384]

        for r in range(8):
            S.op("pool", (lambda e, r=r: e.dma_start(out=w_in_b[r * 128:(r + 1) * 128, :], in_=w_in[r * 128:(r + 1) * 128, :])),
                 writes=[B("w_in_b")], dma=B("w_in_b"))
        for c in range(16):
            S.op("pool", (lambda e, c=c: e.dma_start(out=xTb[:, c * 512:(c + 1) * 512], in_=xT[:, c * 512:(c + 1) * 512])),
                 writes=[B("xTb", c)], dma=B("xTb", c))
        for r in range(8):
            S.op("pool", (lambda e, r=r: e.dma_start(out=w_out_b[r * 128:(r + 1) * 128, :], in_=w_out[r * 128:(r + 1) * 128, :])),
                 writes=[B("w_out_b")], dma=B("w_out_b"))
        w_up_v = w_up.rearrange("(dc p) n -> p dc n", p=128)
        for j in range(NJ):
            for part in range(2):
                c0 = part * DFF + j * 128
                S.op("pool", (lambda e, j=j, part=part, c0=c0: e.dma_start(
                    out=w_up_b[j, :, :, part * 128:(part + 1) * 128], in_=w_up_v[:, :, c0:c0 + 128])),
                    writes=[B("w_up_b", j)], dma=B("w_up_b", j))
        for j in range(NJ):
            S.op("pool", (lambda e, j=j: e.dma_start(out=w_down_b[j * 128:(j + 1) * 128, :], in_=w_down[j * 128:(j + 1) * 128, :])),
                 writes=[B("w_down_b", j)], dma=B("w_down_b", j))

        S.op("sp", lambda e: e.dma_start(out=cstf[:], in_=cst[:, :]), writes=[B("cstf")], dma=B("cstf"))
        S.op("sp", lambda e: e.dma_start(out=flagt[:], in_=flag[0:1, :].partition_broadcast(128)), writes=[B("flagt")], dma=B("flagt"))
        S.op("dve", lambda e: e.tensor_copy(out=trib[:], in_=cstf[:, 0:128]), reads=[B("cstf")], writes=[B("trib")])
        S.op("dve", lambda e: e.tensor_copy(out=identb[:], in_=cstf[:, 128:256]), reads=[B("cstf")], writes=[B("identb")])
        S.op("dve", lambda e: e.memset(onesb[:], 1.0), writes=[B("onesb")])
        S.op("dve", lambda e: e.memset(onesf[:], 1.0), writes=[B("onesf")])

        w_in_v = w_in_b.rearrange("(dc p) n -> p dc n", p=128)
        xTb_v = xTb.rearrange("(dc p) t -> p dc t", p=128)

        def dbg_dump(key, tile_ap, bufl):
            if key in dbg_out:
                S.op("sp", lambda e: e.dma_start(out=dbg_out[key][:, :], in_=tile_ap), reads=bufl, dma=B("dbg", key))

        ab = ExitStack()
        with ab:
            banks = [ab.enter_context(nc.psum_tensor("bank%d" % i, [128, 512], F32)) for i in range(8)]
            KT = sb("KT", [128, SEQ], BF16, ab)
            V = sb("V", [128, NB * 128], BF16, ab)
            QT = sb("QT", [128, NQ], BF16, ab)
            xcs = [sb("xc%d" % i, [128, 8 * 512], BF16, ab) for i in range(2)]
            wq = sb("wq", [128, 8 * 128], BF16, ab)
            wk = sb("wk", [128, 8 * 128], BF16, ab)
            wv = sb("wv", [128, 8 * 128], BF16, ab)
            wf = sb("wf", [128, 8 * 8], BF16, ab)
            pTs = [sb("pT%d" % i, [128, 512], BF16, ab) for i in range(4)]
            rec = sb("rec", [128, 512], F32, ab)
            Fz = sb("Fz", [128, 512], F32, ab)
            Lt = sb("Lt", [128, 512], F32, ab)
            TOT = sb("TOT", [128, 512], F32, ab)
            offs = sb("offs", [128, 512], F32, ab)
            cumL = sb("cumL", [128, 512], F32, ab)
            bfgt = sb("bfgt", [128, 512], F32, ab)
            maskAt = sb("maskAt", [128, NG * 64], F32, ab)
            maskBt = sb("maskBt", [128, NG * 8], F32, ab)
            biasAs = [sb("biasA%d" % i, [128, 128], F32, ab) for i in range(2)]
            biasBb = sb("biasBb", [128, 8 * 5 * 128], BF16, ab)

            S.op("sp", lambda e: e.dma_start(out=bfgt[:], in_=bfg[0:1, :].partition_broadcast(128)), writes=[B("bfgt")], dma=B("bfgt"))
            S.op("sp", lambda e: e.dma_start(out=maskAt[:], in_=maskA[0:1, :].partition_broadcast(128)), writes=[B("maskAt")], dma=B("maskAt"))
            S.op("sp", lambda e: e.dma_start(out=maskBt[:], in_=maskB[0:1, :].partition_broadcast(128)), writes=[B("maskBt")], dma=B("maskBt"))
            S.op("pool", lambda e: e.dma_start(out=biasBb[:], in_=biasB[:, :]), writes=[B("biasBb")], dma=B("biasBb"))
            S.op("dve", lambda e: e.tensor_scalar(out=biasBb[:], in0=biasBb[:], scalar1=8.0, scalar2=None, op0=ALU.mult),
                 reads=[B("biasBb")], writes=[B("biasBb")])

            rot = {"S": 0, "P": 0, "proj": 0, "bA": 0}

            def proj_pass(kind, hp):
                if kind == "A":
                    qc, kc, vc = hp * 128, 512 + hp * 128, 1024 + hp * 128
                    c_start = 0
                else:
                    qc, kc, vc = 1544 + hp * 128, 2056 + hp * 128, 2568 + hp * 128
                    c_start = 6
                for (t, c0, nm) in ((wq, qc, "wq"), (wk, kc, "wk"), (wv, vc, "wv")):
                    S.op("sp", (lambda e, t=t, c0=c0: e.dma_start(out=t[:].rearrange("p (dc n) -> p dc n", dc=8),
                                                                  in_=w_in_v[:, :, c0:c0 + 128])),
                         reads=[B("w_in_b")], writes=[B(nm)], dma=B(nm))
                do_f = (kind == "A" and hp == 0)
                if do_f:
                    S.op("sp", lambda e: e.dma_start(out=wf[:].rearrange("p (dc n) -> p dc n", dc=8), in_=w_in_v[:, :, 1536:1544]),
                         reads=[B("w_in_b")], writes=[B("wf")], dma=B("wf"))
                for c in range(c_start, 16):
                    xc = xcs[c % 2]
                    Bxc = B("xc", c % 2)
                    S.op("sp", (lambda e, xc=xc, c=c: e.dma_start(out=xc[:].rearrange("p (dc t) -> p dc t", dc=8),
                                                                  in_=xTb_v[:, :, c * 512:(c + 1) * 512])),
                         reads=[B("xTb", c)], writes=[Bxc], dma=Bxc)
                    bi = rot["proj"] % 4
                    rot["proj"] += 1
                    ps = banks[bi]
                    for dc in range(8):
                        S.op("pe", (lambda e, ps=ps, xc=xc, dc=dc: e.matmul(ps[:, 0:512], lhsT=wk[:, dc * 128:(dc + 1) * 128],
                                                                         rhs=xc[:, dc * 512:(dc + 1) * 512], start=(dc == 0), stop=(dc == 7))),
                             reads=[B("wk"), Bxc], writes=[Bbank[bi]])
                    S.op("act", (lambda e, ps=ps, c=c: e.copy(out=KT[:, c * 512:(c + 1) * 512], in_=ps[:, 0:512])),
                         reads=[Bbank[bi]], writes=[B("KT", c)])
                    if c >= 7:
                        if c == 7:
                            xs, n, qo = 384, 128, 0
                        else:
                            xs, n, qo = 0, 512, 128 + (c - 8) * 512
                        bi = rot["proj"] % 4
                        rot["proj"] += 1
                        ps = banks[bi]
                        for dc in range(8):
                            S.op("pe", (lambda e, ps=ps, xc=xc, dc=dc, xs=xs, n=n: e.matmul(
                                ps[:, 0:n], lhsT=wq[:, dc * 128:(dc + 1) * 128], rhs=xc[:, dc * 512 + xs:dc * 512 + xs + n],
                                start=(dc == 0), stop=(dc == 7))), reads=[B("wq"), Bxc], writes=[Bbank[bi]])
                        S.op("act", (lambda e, ps=ps, n=n, qo=qo: e.copy(out=QT[:, qo:qo + n], in_=ps[:, 0:n])),
                             reads=[Bbank[bi]], writes=[B("QT", qo)])
                    bi = rot["proj"] % 4
                    rot["proj"] += 1
                    ps = banks[bi]
                    for blk in range(4):
                        for dc in range(8):
                            S.op("pe", (lambda e, ps=ps, xc=xc, dc=dc, blk=blk: e.matmul(
                                ps[:, blk * 128:(blk + 1) * 128], lhsT=xc[:, dc * 512 + blk * 128:dc * 512 + (blk + 1) * 128],
                                rhs=wv[:, dc * 128:(dc + 1) * 128], start=(dc == 0), stop=(dc == 7))),
                                reads=[B("wv"), Bxc], writes=[Bbank[bi]])
                    S.op("dve", (lambda e, ps=ps, c=c: e.tensor_copy(out=V[:, c * 512:(c + 1) * 512], in_=ps[:, 0:512])),
                         reads=[Bbank[bi]], writes=[B("V", c)])
                    if do_f:
                        bi = rot["proj"] % 4
                        rot["proj"] += 1
                        ps = banks[bi]
                        for blk in range(4):
                            for dc in range(8):
                                S.op("pe", (lambda e, ps=ps, xc=xc, dc=dc, blk=blk: e.matmul(
                                    ps[:, blk * 8:(blk + 1) * 8], lhsT=xc[:, dc * 512 + blk * 128:dc * 512 + (blk + 1) * 128],
                                    rhs=wf[:, dc * 8:(dc + 1) * 8], start=(dc == 0), stop=(dc == 7))),
                                    reads=[B("wf"), Bxc], writes=[Bbank[bi]])
                        S.op("dve", (lambda e, ps=ps, c=c: e.tensor_copy(out=Fz[:, c * 32:(c + 1) * 32], in_=ps[:, 0:32])),
                             reads=[Bbank[bi]], writes=[B("Fz")])

            def cum_compute():
                S.op("dve", lambda e: e.tensor_tensor(out=Fz[:], in0=Fz[:], in1=bfgt[:], op=ALU.add), reads=[B("Fz"), B("bfgt")], writes=[B("Fz")])
                S.op("dve", lambda e: e.tensor_scalar_max(out=Fz[:], in0=Fz[:], scalar1=-30.0), reads=[B("Fz")], writes=[B("Fz")])
                S.op("act", lambda e: e.activation(out=Lt[:], in_=Fz[:], func=AF.Exp, scale=-1.0), reads=[B("Fz")], writes=[B("Lt")])
                S.op("act", lambda e: e.activation(out=Lt[:], in_=Lt[:], func=AF.Ln, bias=1.0, scale=1.0), reads=[B("Lt")], writes=[B("Lt")])
                S.op("pe", lambda e: e.matmul(banks[5][:, 0:512], lhsT=utri, rhs=Lt[:], start=True, stop=True),
                     reads=[B("Lt"), B("cstf")], writes=[Bbank[5]])
                S.op("pe", lambda e: e.matmul(banks[6][:, 0:512], lhsT=onesf[:], rhs=Lt[:], start=True, stop=True),
                     reads=[B("Lt"), B("onesf")], writes=[Bbank[6]])
                S.op("dve", lambda e: e.tensor_copy(out=TOT[:], in_=banks[6][:, 0:512]), reads=[Bbank[6]], writes=[B("TOT")])
                S.op("dve", lambda e: e.memset(offs[:, 0:8], 0.0), writes=[B("offs")])
                for kb in range(1, NB):
                    S.op("dve", (lambda e, kb=kb: e.tensor_tensor(out=offs[:, kb * 8:(kb + 1) * 8], in0=offs[:, (kb - 1) * 8:kb * 8],
                                                                  in1=TOT[:, (kb - 1) * 8:kb * 8], op=ALU.add)),
                         reads=[B("offs"), B("TOT")], writes=[B("offs")])
                S.op("dve", lambda e: e.tensor_tensor(out=cumL[:], in0=banks[5][:, 0:512], in1=offs[:], op=ALU.add),
                     reads=[Bbank[5], B("offs")], writes=[B("cumL")])

            cumL_v = cumL[:].rearrange("p (kb h) -> p h kb", h=8)

            def attn_pass(kind, hp):
                oc = hp if kind == "A" else 4 + hp
                steps = []
                for g in range(NG):
                    kbs, N, qoff = ginfo(g)
                    nq = N // 128
                    nsteps = kbs + nq if kind == "A" else nq + 4
                    for st in range(nsteps):
                        steps.append((g, st, nsteps))
                ctx = {}

                def front(g, st, nsteps):
                    kbs, N, qoff = ginfo(g)
                    nq = N // 128
                    if st == 0 and kind == "A":
                        kref = kbs + 2 if nq == 4 else kbs
                        bA = biasAs[rot["bA"] % 2]
                        BbA = B("biasA", rot["bA"] % 2)
                        rot["bA"] += 1
                        ctx[g] = (bA, BbA)
                        for h in range(2):
                            H = 2 * hp + h
                            S.op("dve", (lambda e, bA=bA, h=h, H=H, nsteps=nsteps, kref=kref, g=g: e.scalar_tensor_tensor(
                                out=bA[:, h * 64:h * 64 + nsteps], in0=cumL_v[:, H, 0:nsteps],
                                scalar=offs[:, kref * 8 + H:kref * 8 + H + 1], in1=maskAt[:, g * 64:g * 64 + nsteps],
                                op0=ALU.subtract, op1=ALU.add)),
                                reads=[B("cumL"), B("offs"), B("maskAt")], writes=[BbA])
                    if kind == "A":
                        kb = st
                        i_lo, i_hi = max(0, kb - kbs), nq - 1
                    else:
                        kb = kbs - 4 + st
                        i_lo, i_hi = max(0, st - 4), min(nq - 1, st)
                    q_lo, q_hi = i_lo * 128, (i_hi + 1) * 128
                    Bk = B("KT", kb // 4)
                    info = []
                    for h in range(2):
                        H = 2 * hp + h
                        r0, r1 = h * 64, (h + 1) * 64
                        si = rot["S"] % 4
                        rot["S"] += 1
                        ps = banks[si]
                        extra = []
                        if kind == "A":
                            if kb >= kbs:
                                extra.append((i_lo, trib[:]))
                        else:
                            for i in range(i_lo, i_hi + 1):
                                o = i - (st - 4)
                                extra.append((i, biasBb[:, (H * 5 + o) * 128:(H * 5 + o + 1) * 128]))
                        S.op("pe", (lambda e, ps=ps, r0=r0, r1=r1, kb=kb, q_lo=q_lo, q_hi=q_hi, qoff=qoff, last=(not extra): e.matmul(
                            ps[:, q_lo:q_hi], lhsT=KT[r0:r1, kb * 128:(kb + 1) * 128], rhs=QT[r0:r1, qoff + q_lo:qoff + q_hi],
                            start=True, stop=last)), reads=[Bk, B("QT", qoff)], writes=[Bbank[si]])
                        for n_, (i, bt) in enumerate(extra):
                            S.op("pe", (lambda e, ps=ps, i=i, bt=bt, last=(n_ == len(extra) - 1): e.matmul(
                                ps[:, i * 128:(i + 1) * 128], lhsT=identb[:], rhs=bt, start=False, stop=last)),
                                reads=[B("identb"), B("trib"), B("biasBb")], writes=[Bbank[si]])
                        info.append((si, ps))
                    pis = []
                    for h in range(2):
                        si, ps = info[h]
                        pi = rot["P"] % 4
                        rot["P"] += 1
                        pT = pTs[pi]
                        if kind == "A":
                            bA, BbA = ctx[g]
                            bias_ap = bA[:, h * 64 + kb:h * 64 + kb + 1]
                            rd = [Bbank[si], BbA]
                        else:
                            bias_ap = maskBt[:, g * 8 + st:g * 8 + st + 1]
                            rd = [Bbank[si], B("maskBt")]
                        S.op("act", (lambda e, pT=pT, ps=ps, q_lo=q_lo, q_hi=q_hi, bias_ap=bias_ap: e.activation(
                            out=pT[:, q_lo:q_hi], in_=ps[:, q_lo:q_hi], func=AF.Exp, bias=bias_ap, scale=0.125)),
                            reads=rd, writes=[B("pT", pi)])
                        pis.append(pi)
                    return (kb, q_lo, q_hi, pis)

                def back(g, st, nsteps, fr):
                    kbs, N, qoff = ginfo(g)
                    kb, c0, c1, pis = fr
                    Bv = B("V", kb // 4)
                    bo, bd = (4, 5) if g % 2 == 0 else (6, 7)
                    po, pd = banks[bo], banks[bd]
                    first = (st == 0)
                    final = (st == nsteps - 1)
                    for h in range(2):
                        r0, r1 = h * 64, (h + 1) * 64
                        pT = pTs[pis[h]]
                        S.op("pe", (lambda e, r0=r0, r1=r1, kb=kb, h=h, pT=pT, po=po: e.matmul(
                            po[r0:r1, c0:c1], lhsT=V[:, kb * 128 + h * 64:kb * 128 + (h + 1) * 64], rhs=pT[:, c0:c1],
                            start=first, stop=final, skip_group_check=True)), reads=[Bv, B("pT", pis[h])], writes=[Bbank[bo]])
                    for h in range(2):
                        r0, r1 = h * 64, (h + 1) * 64
                        pT = pTs[pis[h]]
                        S.op("pe", (lambda e, r0=r0, r1=r1, pT=pT, pd=pd: e.matmul(
                            pd[r0:r1, c0:c1], lhsT=onesb[:, 0:64], rhs=pT[:, c0:c1],
                            start=first, stop=final, skip_group_check=True)), reads=[B("onesb"), B("pT", pis[h])], writes=[Bbank[bd]])
                    if final:
                        S.op("dve", (lambda e, N=N, pd=pd: e.reciprocal(out=rec[:, 0:N], in_=pd[:, 0:N])), reads=[Bbank[bd]], writes=[B("rec")])
                        S.op("dve", (lambda e, N=N, qoff=qoff, po=po: e.tensor_tensor(out=OT[oc][:, qoff:qoff + N], in0=po[:, 0:N],
                                                                                      in1=rec[:, 0:N], op=ALU.mult)),
                             reads=[Bbank[bo], B("rec")], writes=[B("OT", oc, g)])

                prev = None
                for i in range(len(steps) + 1):
                    cur = None
                    if i < len(steps):
                        cur = front(*steps[i])
                    if prev is not None:
                        back(*steps[i - 1], prev)
                    prev = cur

            for hp in range(STAGE["hpA"]):
                proj_pass("A", hp)
                if hp == 0:
                    cum_compute()
                    dbg_dump("cumL", cumL[:], [B("cumL")])
                attn_pass("A", hp)
            for hp in range(STAGE["hpB"]):
                proj_pass("B", hp)
                attn_pass("B", hp)
            for c in range(8):
                dbg_dump("OT%d" % c, OT[c][:], [B("OT", c, g) for g in range(NG)])
            S.flush(nc, es)

        pc = ExitStack()
        if not STAGE["C"]:
            S.op("sp", None, reads=[])
            return nc
        with pc:
            banks = [pc.enter_context(nc.psum_tensor("cbank%d" % i, [128, 512], F32)) for i in range(7)]
            bankT = pc.enter_context(nc.psum_tensor("bankT", [128, 1024], BF16))
            woutb = sb("woutb", [128, 8 * 1024], BF16, pc)
            lnt = sb("lnt", [128, 4 * 1024], F32, pc)
            cpt = sb("cpt", [128, 44 * 4], F32, pc)
            tails = sb("tails", [128, 44 * 2], F32, pc)
            x1g = sb("x1g", [128, 4 * 1024], F32, pc)
            xblk = [sb("xblk%d" % i, [128, 1024], F32, pc) for i in range(2)]
            obuf = [sb("obuf%d" % i, [128, 1024], F32, pc) for i in range(2)]
            x1b = [sb("x1b%d" % i, [128, 1024], BF16, pc) for i in range(2)]
            junkb = sb("junkb", [128, 1024], BF16, pc)
            x1T = sb("x1T", [128, 8 * 512], BF16, pc)
            hT = sb("hT", [128, NJ * 512], BF16, pc)
            av = [sb("av%d" % i, [128, 512], F32, pc) for i in range(2)]
            ag = [sb("ag%d" % i, [128, 512], F32, pc) for i in range(2)]
            wus = [sb("wu%d" % i, [128, 8 * 256], BF16, pc) for i in range(3)]
            wds = [sb("wd%d" % i, [128, 1024], BF16, pc) for i in range(2)]
            stt = sb("stt", [128, 4 * 8], F32, pc)

            S.op("sp", lambda e: e.dma_start(out=woutb[:].rearrange("p (c n) -> p c n", c=8),
                                             in_=w_out_b.rearrange("(c p) n -> p c n", p=128)),
                 reads=[B("w_out_b")], writes=[B("woutb")], dma=B("woutb"))
            for i in range(4):
                S.op("sp", (lambda e, i=i: e.dma_start(out=lnt[:, i * 1024:(i + 1) * 1024], in_=lnp[i:i + 1, :].partition_broadcast(128))),
                     writes=[B("lnt")], dma=B("lnt"))
            S.op("sp", lambda e: e.dma_start(out=cpt[:], in_=convp[:, :]), writes=[B("cpt")], dma=B("cpt"))

            rc = {"x": 0, "wu": 0, "wd": 0, "u": 0, "o": 0, "up": 0, "m": 0, "ln": 0, "xb": 0}

            def layer_norm(buf, Bbuf, gi):
                k = rc["ln"] % 4
                rc["ln"] += 1
                st = stt[:, k * 8:(k + 1) * 8]
                Bst = B("stt", k)
                S.op("dve", lambda e: e.memset(st[:, 0:2], 0.0), writes=[Bst])
                S.op("act", lambda e: e.activation(out=junkb[:], in_=buf, func=AF.Identity, accum_out=st[:, 0:1]),
                     reads=[Bbuf, Bst], writes=[B("junkb"), Bst])
                S.op("act", lambda e: e.activation(out=junkb[:], in_=buf, func=AF.Square, accum_out=st[:, 1:2]),
                     reads=[Bbuf, Bst], writes=[B("junkb"), Bst])
                S.op("dve", lambda e: e.tensor_scalar(out=st[:, 2:3], in0=st[:, 0:1], scalar1=1.0 / D, scalar2=None, op0=ALU.mult),
                     reads=[Bst], writes=[Bst])
                S.op("dve", lambda e: e.tensor_tensor(out=st[:, 3:4], in0=st[:, 2:3], in1=st[:, 2:3], op=ALU.mult),
                     reads=[Bst], writes=[Bst])
                S.op("dve", lambda e: e.scalar_tensor_tensor(out=st[:, 4:5], in0=st[:, 1:2], scalar=1.0 / D, in1=st[:, 3:4],
                                                             op0=ALU.mult, op1=ALU.subtract), reads=[Bst], writes=[Bst])
                S.op("act", lambda e: e.activation(out=st[:, 6:7], in_=st[:, 4:5], func=AF.Sqrt, bias=st[:, 5:6], scale=1.0),
                     reads=[Bst], writes=[Bst])
                S.op("dve", lambda e: e.reciprocal(out=st[:, 7:8], in_=st[:, 6:7]), reads=[Bst], writes=[Bst])
                S.op("dve", lambda e: e.scalar_tensor_tensor(out=buf, in0=buf, scalar=st[:, 2:3], in1=lnt[:, gi * 1024:(gi + 1) * 1024],
                                                             op0=ALU.subtract, op1=ALU.mult), reads=[Bbuf, Bst, B("lnt")], writes=[Bbuf])
                S.op("dve", lambda e: e.scalar_tensor_tensor(out=buf, in0=buf, scalar=st[:, 7:8], in1=lnt[:, (gi + 1) * 1024:(gi + 2) * 1024],
                                                             op0=ALU.mult, op1=ALU.add), reads=[Bbuf, Bst, B("lnt")], writes=[Bbuf])

            for k in range(4):
                S.op("dve", (lambda e, k=k: e.memset(stt[:, k * 8 + 5:k * 8 + 6], EPS)), writes=[B("stt", k)])

            for g in range(NG):
                kbs, N, qoff = ginfo(g)
                nq = N // 128
                for tb in range(nq):
                    t0 = qoff + tb * 128
                    xi = rc["x"] % 2
                    rc["x"] += 1
                    xb = xblk[xi]
                    S.op("sp", (lambda e, xb=xb, t0=t0: e.dma_start(out=xb[:], in_=xo[t0:t0 + 128, :])), writes=[B("xblk", xi)], dma=B("xblk", xi))
                    for half in range(2):
                        bi = rc["m"] % 3
                        rc["m"] += 1
                        ps = banks[bi]
                        for c in range(8):
                            S.op("pe", (lambda e, ps=ps, c=c, t0=t0, half=half: e.matmul(
                                ps[:, 0:512], lhsT=OT[c][:, t0:t0 + 128], rhs=woutb[:, c * 1024 + half * 512:c * 1024 + (half + 1) * 512],
                                start=(c == 0), stop=(c == 7))), reads=[B("OT", c, g), B("woutb")], writes=[Bbank[bi]])
                        S.op("dve", (lambda e, ps=ps, xb=xb, half=half, tb=tb: e.scalar_tensor_tensor(
                            out=x1g[:, tb * 1024 + half * 512:tb * 1024 + (half + 1) * 512], in0=xb[:, half * 512:(half + 1) * 512], scalar=ALPHA,
                            in1=ps[:, 0:512], op0=ALU.mult, op1=ALU.add)), reads=[Bbank[bi], B("xblk", xi)], writes=[B("x1g", tb)])
                for tb in range(nq):
                    x1 = x1g[:, tb * 1024:(tb + 1) * 1024]
                    layer_norm(x1, B("x1g", tb), 0)
                    xk = rc["xb"] % 2
                    rc["xb"] += 1
                    xbb = x1b[xk]
                    S.op("act", (lambda e, x1=x1, xbb=xbb: e.copy(out=xbb[:], in_=x1)), reads=[B("x1g", tb)], writes=[B("x1b", xk)])
                    for dc in range(8):
                        S.op("pe", (lambda e, dc=dc, xbb=xbb: e.transpose(bankT[:, dc * 128:(dc + 1) * 128], xbb[:, dc * 128:(dc + 1) * 128], identb[:])),
                             reads=[B("x1b", xk), B("identb")], writes=[BbankT])
                    S.op("act", (lambda e, N=N, tb=tb: e.copy(
                        out=x1T[:, 0:8 * N].rearrange("p (dc t) -> p dc t", dc=8)[:, :, tb * 128:(tb + 1) * 128],
                        in_=bankT[:, 0:1024].rearrange("p (dc t) -> p dc t", dc=8))), reads=[BbankT], writes=[B("x1T")])
                for j in range(NJ):
                    wi = rc["wu"] % 3
                    rc["wu"] += 1
                    wu = wus[wi]
                    S.op("sp", (lambda e, wu=wu, j=j: e.dma_start(out=wu[:].rearrange("p (dc n) -> p dc n", dc=8), in_=w_up_b[j])),
                         reads=[B("w_up_b", j)], writes=[B("wu", wi)], dma=B("wu", wi))
                    ui = rc["up"] % 2
                    rc["up"] += 1
                    bv, bg = 3 + 2 * ui, 4 + 2 * ui
                    for (bi, off) in ((bv, 0), (bg, 128)):
                        ps = banks[bi]
                        for dc in range(8):
                            S.op("pe", (lambda e, ps=ps, wu=wu, dc=dc, off=off, N=N: e.matmul(
                                ps[:, 0:N], lhsT=wu[:, dc * 256 + off:dc * 256 + off + 128], rhs=x1T[:, dc * N:(dc + 1) * N],
                                start=(dc == 0), stop=(dc == 7))), reads=[B("wu", wi), B("x1T")], writes=[Bbank[bi]])
                    if g == 0:
                        S.op("dve", (lambda e, j=j, bv=bv, N=N: e.tensor_scalar(out=tails[:, j * 2:j * 2 + 2], in0=banks[bv][:, N - 2:N],
                                                                                 scalar1=flagt[:, 0:1], scalar2=None, op0=ALU.mult)),
                             reads=[Bbank[bv], B("flagt")], writes=[B("tails", j)])
                        S.op("dve", (lambda e, j=j, bg=bg, N=N: e.tensor_scalar(out=tails[:, (NJ + j) * 2:(NJ + j) * 2 + 2], in0=banks[bg][:, N - 2:N],
                                                                                 scalar1=flagt[:, 0:1], scalar2=None, op0=ALU.mult)),
                             reads=[Bbank[bg], B("flagt")], writes=[B("tails", NJ + j)])
                        continue
                    k = rc["u"] % 2
                    rc["u"] += 1
                    for (bi, jj, acc, accn) in ((bv, j, av[k], "av"), (bg, NJ + j, ag[k], "ag")):
                        Bacc = B(accn, k)
                        ps = banks[bi]
                        w0 = cpt[:, jj * 4:jj * 4 + 1]
                        w1 = cpt[:, jj * 4 + 1:jj * 4 + 2]
                        w2 = cpt[:, jj * 4 + 2:jj * 4 + 3]
                        cb_ = cpt[:, jj * 4 + 3:jj * 4 + 4]
                        tl = tails[:, jj * 2:jj * 2 + 2]
                        Bt = B("tails", jj)
                        S.op("act", (lambda e, acc=acc, ps=ps, w2=w2, cb_=cb_: e.activation(out=acc[:], in_=ps[:, 0:512], func=AF.Identity,
                                                                                          bias=cb_, scale=w2)),
                             reads=[Bbank[bi], B("cpt")], writes=[Bacc])
                        S.op("dve", (lambda e, acc=acc, ps=ps, w1=w1: e.scalar_tensor_tensor(out=acc[:, 1:512], in0=ps[:, 0:511], scalar=w1,
                                                                                           in1=acc[:, 1:512], op0=ALU.mult, op1=ALU.add)),
                             reads=[Bbank[bi], B("cpt"), Bacc], writes=[Bacc])
                        S.op("dve", (lambda e, acc=acc, ps=ps, w0=w0: e.scalar_tensor_tensor(out=acc[:, 2:512], in0=ps[:, 0:510], scalar=w0,
                                                                                           in1=acc[:, 2:512], op0=ALU.mult, op1=ALU.add)),
                             reads=[Bbank[bi], B("cpt"), Bacc], writes=[Bacc])
                        S.op("dve", (lambda e, acc=acc, tl=tl, w1=w1: e.scalar_tensor_tensor(out=acc[:, 0:1], in0=tl[:, 1:2], scalar=w1,
                                                                                           in1=acc[:, 0:1], op0=ALU.mult, op1=ALU.add)),
                             reads=[Bt, B("cpt"), Bacc], writes=[Bacc])
                        S.op("dve", (lambda e, acc=acc, tl=tl, w0=w0: e.scalar_tensor_tensor(out=acc[:, 0:2], in0=tl[:, 0:2], scalar=w0,
                                                                                           in1=acc[:, 0:2], op0=ALU.mult, op1=ALU.add)),
                             reads=[Bt, B("cpt"), Bacc], writes=[Bacc])
                        S.op("dve", (lambda e, tl=tl, ps=ps: e.tensor_copy(out=tl, in_=ps[:, 510:512])), reads=[Bbank[bi]], writes=[Bt])
                    S.op("act", (lambda e, k=k: e.activation(out=ag[k][:], in_=ag[k][:], func=AF.Gelu_apprx_tanh)),
                         reads=[B("ag", k)], writes=[B("ag", k)])
                    S.op("dve", (lambda e, k=k, j=j: e.tensor_tensor(out=hT[:, j * 512:(j + 1) * 512], in0=av[k][:], in1=ag[k][:], op=ALU.mult)),
                         reads=[B("av", k), B("ag", k)], writes=[B("hT", j)])
                if g == 0:
                    continue
                for hh in range(2):
                    for j in range(NJ):
                        wi = rc["wd"] % 2
                        rc["wd"] += 1
                        wd = wds[wi]
                        S.op("sp", (lambda e, wd=wd, j=j: e.dma_start(out=wd[:], in_=w_down_b[j * 128:(j + 1) * 128, :])),
                             reads=[B("w_down_b", j)], writes=[B("wd", wi)], dma=B("wd", wi))
                        for tbl in range(2):
                            tb = 2 * hh + tbl
                            for half in range(2):
                                bi = 3 + tbl * 2 + half
                                S.op("pe", (lambda e, bi=bi, wd=wd, j=j, tb=tb, half=half: e.matmul(
                                    banks[bi][:, 0:512], lhsT=hT[:, j * 512 + tb * 128:j * 512 + (tb + 1) * 128],
                                    rhs=wd[:, half * 512:(half + 1) * 512], start=(j == 0), stop=(j == NJ - 1))),
                                    reads=[B("hT", j), B("wd", wi)], writes=[Bbank[bi]])
                    for tbl in range(2):
                        tb = 2 * hh + tbl
                        oi = rc["o"] % 2
                        rc["o"] += 1
                        ob = obuf[oi]
                        for half in range(2):
                            bi = 3 + tbl * 2 + half
                            S.op("dve", (lambda e, bi=bi, tb=tb, half=half, ob=ob: e.scalar_tensor_tensor(
                                out=ob[:, half * 512:(half + 1) * 512], in0=x1g[:, tb * 1024 + half * 512:tb * 1024 + (half + 1) * 512],
                                scalar=ALPHA, in1=banks[bi][:, 0:512], op0=ALU.mult, op1=ALU.add)),
                                reads=[Bbank[bi], B("x1g", tb)], writes=[B("obuf", oi)])
                        layer_norm(ob[:], B("obuf", oi), 2)
                        r0 = (g - 1) * 512 + tb * 128
                        S.op("pool", (lambda e, ob=ob, r0=r0: e.dma_start(out=out[r0:r0 + 128, :], in_=ob[:])),
                             reads=[B("obuf", oi)], dma=B("obuf_st", oi))
            fin = [B("obuf_st", 0), B("obuf_st", 1)] + [B("dbg", k) for k in dbg_out]
            for b in fin:
                if b.cnt:
                    b.w = ("d", b, b.cnt)
            S.op("sp", None, reads=[b for b in fin if b.cnt])
            S.flush(nc, es)
    return nc


def host_inputs(x, w_in, b_forget, rel_bias, w_out, ln1_g, ln1_b, w_up, conv_w, conv_b, w_down, ln2_g, ln2_b):
    f = np.float32
    x = np.asarray(x, f)
    w_in0, w_out0, w_up0, w_down0 = (np.ascontiguousarray(np.asarray(a, f)[0]) for a in (w_in, w_out, w_up, w_down))
    lnp = np.ascontiguousarray(np.stack([np.asarray(a, f)[0] for a in (ln1_g, ln1_b, ln2_g, ln2_b)]))
    cw = np.asarray(conv_w, f)[0]
    cb = np.asarray(conv_b, f)[0]
    convp = np.zeros((128, 44, 4), f)
    for j in range(44):
        base = j * 128 if j < NJ else DFF + (j - NJ) * 128
        convp[:, j, 0:3] = cw[:, base:base + 128].T
        convp[:, j, 3] = cb[base:base + 128]
    convp = np.ascontiguousarray(convp.reshape(128, 176))
    bfgv = np.ascontiguousarray(np.tile(np.asarray(b_forget, f)[0], NB)[None, :])
    rb = np.asarray(rel_bias, f)[0]
    kk = np.arange(128)[:, None]
    qq = np.arange(128)[None, :]
    tiles = np.zeros((128, 8, 5, 128), f)
    for o in range(5):
        idx = np.clip(qq - kk + 128 * o, -128, 128) + 128
        t = rb[:, idx]
        t = np.transpose(t, (1, 0, 2)).copy()
        if o == 0:
            t[:, :, :][np.broadcast_to(((kk >= 64) & (qq < 64))[:, None, :], t.shape)] = NEG
        if o == 4:
            t[np.broadcast_to(((kk < 64) & (qq >= 64))[:, None, :], t.shape)] = NEG
        tiles[:, :, o, :] = t
    biasB = np.ascontiguousarray(tiles.reshape(128, 8 * 5 * 128))
    tri = np.where(kk <= qq, 0.0, NEG).astype(f)
    ident = np.eye(128, dtype=f)
    utri = (kk <= qq).astype(f)
    cst = np.ascontiguousarray(np.concatenate([tri, ident, utri], axis=1))
    maps = []
    for core in range(8):
        b, hi = core // 2, core % 2
        own = x[b, hi * HALF:(hi + 1) * HALF]
        prev = x[b, 0:HALF] if hi else own
        store = np.concatenate([prev, own], axis=0)
        xTm = np.ascontiguousarray(store.T)
        xom = np.ascontiguousarray(store[31 * 128:])
        mA = np.zeros((NG, 64), f)
        mB = np.zeros((NG, 8), f)
        if not hi:
            for g in range(NG):
                kbs, N, _ = ginfo(g)
                lim = 31 if g == 0 else 32
                mA[g, :lim] = NEG
                for st in range(8):
                    if kbs - 4 + st < lim:
                        mB[g, st] = NEG
        maps.append({"xT": xTm, "xo": xom, "w_in": w_in0, "w_out": w_out0, "w_up": w_up0, "w_down": w_down0,
                     "lnp": lnp, "convp": convp, "bfg": bfgv, "biasB": biasB,
                     "maskA": np.ascontiguousarray(mA.reshape(1, -1)), "maskB": np.ascontiguousarray(mB.reshape(1, -1)),
                     "flag": np.full((1, 1), float(hi), f), "cst": cst})
    return maps


_NC = None


def kernel(**inputs):
    global _NC
    maps = host_inputs(**inputs)
    if _NC is None:
        _NC = build_nc()
    res = run_bass_kernel_spmd(_NC, maps, core_ids=list(range(8)))
    outp = np.empty((4, SEQ, D), np.float32)
    for core in range(8):
        b, hi = core // 2, core % 2
        outp[b, hi * HALF:(hi + 1) * HALF] = np.asarray(res.results[core]["out"], np.float32)
    kernel.last_results = res
    return outp
```

```python
import numpy as np
from contextlib import ExitStack
import concourse.bass as bass
import concourse.mybir as mybir
from concourse.bass_utils import run_bass_kernel_spmd

F32, BF16 = mybir.dt.float32, mybir.dt.bfloat16
AF = mybir.ActivationFunctionType
ALU = mybir.AluOpType

D = 1024
SEQ = 8192
HALF = 4096
NB = 64
NG = 9
NQ = 4224
DFF = 2816
NJ = 22
PC = 3080
ALPHA = float(2.0 ** 0.25)
EPS = 1e-5
NEG = -30000.0
DEBUG = {}
STAGE = {"hpA": 4, "hpB": 4, "C": True}


def ginfo(g):
    if g == 0:
        return 31, 128, 0
    return 32 + 4 * (g - 1), 512, 128 + 512 * (g - 1)


class Buf:
    __slots__ = ("name", "w", "r", "sem", "cnt")

    def __init__(self, name):
        self.name = name
        self.w = None
        self.r = {}
        self.sem = None
        self.cnt = 0


class Op:
    __slots__ = ("fn", "waits", "signal", "dma")

    def __init__(self, fn, waits, dma):
        self.fn = fn
        self.waits = waits
        self.signal = False
        self.dma = dma


class Sched:
    ENG = ("pe", "act", "dve", "pool", "sp")

    def __init__(self):
        self.ops = {e: [] for e in self.ENG}
        self.bufs = {}
        self.dmabufs = []

    def B(self, *key):
        b = self.bufs.get(key)
        if b is None:
            b = Buf(key)
            self.bufs[key] = b
        return b

    def op(self, eng, fn, reads=(), writes=(), dma=None):
        deps = set()
        for b in reads:
            if b.w is not None:
                deps.add(b.w)
        for b in writes:
            if b.w is not None:
                deps.add(b.w)
            deps.update(b.r.values())
        seq = len(self.ops[eng])
        if dma is not None:
            if dma.cnt == 0:
                self.dmabufs.append(dma)
            dma.cnt += 16
            ev = ("d", dma, dma.cnt)
            rkey = ("d", id(dma))
        else:
            ev = ("e", eng, seq)
            rkey = eng
        waits = []
        for d in deps:
            if d[0] == "e" and d[1] == eng and (eng == "pe" or dma is not None and False):
                continue
            if d[0] == "e":
                fr = getattr(self, "frozen", None)
                if fr is not None and d[2] <= fr[d[1]]:
                    if not (self.ops[d[1]][d[2]].signal and self.ops[d[1]][d[2]].fn is not None):
                        continue
                else:
                    self.ops[d[1]][d[2]].signal = True
            waits.append(d)
        for b in reads:
            b.r[rkey] = ev
        for b in writes:
            b.w = ev
            b.r = {}
        self.ops[eng].append(Op(fn, waits, dma))
        return ev

    def flush(self, nc, es):
        if not hasattr(self, "sems"):
            self.sems = {e: es.enter_context(nc.semaphore("sem_" + e)) for e in self.ENG}
            self.done = {e: 0 for e in self.ENG}
            self.frozen = {e: -1 for e in self.ENG}
            self.nds = 0
        evs = []
        for e in self.ENG:
            n = len(self.ops[e])
            if n:
                j = n - 1
                while j >= 0 and (self.ops[e][j].fn is None or self.ops[e][j].dma is not None):
                    j -= 1
                if j >= 0:
                    self.ops[e][j].signal = True
                    evs.append(("e", e, j))
        for b in self.dmabufs:
            evs.append(("d", b, b.cnt))
        for e in self.ENG:
            self.ops[e].append(Op(None, list(evs), None))
        for b in self.dmabufs:
            if b.sem is None:
                b.sem = es.enter_context(nc.semaphore("dsem%d" % self.nds))
                self.nds += 1
        cum = {}
        for e in self.ENG:
            c = 0
            arr = []
            for o in self.ops[e]:
                if o.signal and o.dma is None and o.fn is not None:
                    c += 1
                arr.append(c)
            cum[e] = arr
        sems = self.sems
        sched = self
        lo = dict(self.done)
        if not hasattr(self, "waited"):
            self.waited = {e: {} for e in self.ENG}

        def replay(e, eobj):
            waited = sched.waited[e]
            for o in sched.ops[e][lo[e]:]:
                for d in o.waits:
                    if d[0] == "e":
                        sem, val = sems[d[1]], cum[d[1]][d[2]]
                    else:
                        sem, val = d[1].sem, d[2]
                    k = id(sem)
                    if waited.get(k, 0) >= val:
                        continue
                    waited[k] = val
                    eobj.wait_ge(sem, val)
                if o.fn is None:
                    continue
                ins = o.fn(eobj)
                if o.dma is not None:
                    ins.then_inc(o.dma.sem, 16)
                elif o.signal:
                    ins.then_inc(sems[e], 1)

        with nc.Block() as block:
            @block.tensor
            def _(t):
                replay("pe", t)

            @block.scalar
            def _(t):
                replay("act", t)

            @block.vector
            def _(t):
                replay("dve", t)

            @block.gpsimd
            def _(t):
                replay("pool", t)

            @block.sync
            def _(t):
                replay("sp", t)
        for e in self.ENG:
            self.done[e] = len(self.ops[e])
            self.frozen[e] = len(self.ops[e]) - 1


def build_nc():
    nc = bass.Bass("TRN2", target_bir_lowering=False)
    S = Sched()
    B = S.B

    def din(name, shape):
        return nc.dram_tensor(name, shape, F32, kind="ExternalInput").ap()

    xT = din("xT", [D, SEQ])
    xo = din("xo", [NQ, D])
    w_in = din("w_in", [D, PC])
    w_out = din("w_out", [D, D])
    w_up = din("w_up", [D, 2 * DFF])
    w_down = din("w_down", [DFF, D])
    lnp = din("lnp", [4, D])
    convp = din("convp", [128, 44 * 4])
    bfg = din("bfg", [1, 512])
    biasB = din("biasB", [128, 8 * 5 * 128])
    maskA = din("maskA", [1, NG * 64])
    maskB = din("maskB", [1, NG * 8])
    flag = din("flag", [1, 1])
    cst = din("cst", [128, 3 * 128])
    out = nc.dram_tensor("out", [HALF, D], F32, kind="ExternalOutput").ap()

    xTb = nc.dram_tensor("xTb", [D, SEQ], BF16).ap()
    w_in_b = nc.dram_tensor("w_in_b", [D, PC], BF16).ap()
    w_out_b = nc.dram_tensor("w_out_b", [D, D], BF16).ap()
    w_up_b = nc.dram_tensor("w_up_b", [NJ, 128, 8, 256], BF16).ap()
    w_down_b = nc.dram_tensor("w_down_b", [DFF, D], BF16).ap()

    dbg_out = {}
    for k, shp in DEBUG.items():
        dbg_out[k] = nc.dram_tensor("dbg_" + k, list(shp[0]), shp[1], kind="ExternalOutput").ap()

    es = ExitStack()
    with es:
        def sb(name, shape, dt, stack=es):
            return stack.enter_context(nc.sbuf_tensor(name, shape, dt))

        Bbank = [B("bank", i) for i in range(8)]
        BbankT = B("bankT")

        OT = [sb("OT%d" % c, [128, NQ], BF16) for c in range(8)]
        cstf = sb("cstf", [128, 384], F32)
        trib = sb("trib", [128, 128], BF16)
        identb = sb("identb", [128, 128], BF16)
        onesb = sb("onesb", [128, 128], BF16)
        onesf = sb("onesf", [128, 128], F32)
        flagt = sb("flagt", [128, 1], F32)
        utri = cstf[:, 256:384]

        def register_casts():
            for r in range(8):
                S.op("pool", (lambda e, r=r: e.dma_start(out=w_in_b[r * 128:(r + 1) * 128, :], in_=w_in[r * 128:(r + 1) * 128, :])),
                     writes=[B("w_in_b")], dma=B("w_in_b"))
            for c in range(16):
                S.op("pool", (lambda e, c=c: e.dma_start(out=xTb[:, c * 512:(c + 1) * 512], in_=xT[:, c * 512:(c + 1) * 512])),
                     writes=[B("xTb", c)], dma=B("xTb", c))
            for r in range(8):
                S.op("pool", (lambda e, r=r: e.dma_start(out=w_out_b[r * 128:(r + 1) * 128, :], in_=w_out[r * 128:(r + 1) * 128, :])),
                     writes=[B("w_out_b")], dma=B("w_out_b"))
            w_up_v = w_up.rearrange("(dc p) n -> p dc n", p=128)
            for j in range(NJ):
                for part in range(2):
                    c0 = part * DFF + j * 128
                    S.op("pool", (lambda e, j=j, part=part, c0=c0: e.dma_start(
                        out=w_up_b[j, :, :, part * 128:(part + 1) * 128], in_=w_up_v[:, :, c0:c0 + 128])),
                        writes=[B("w_up_b", j)], dma=B("w_up_b", j))
            for j in range(NJ):
                S.op("pool", (lambda e, j=j: e.dma_start(out=w_down_b[j * 128:(j + 1) * 128, :], in_=w_down[j * 128:(j + 1) * 128, :])),
                     writes=[B("w_down_b", j)], dma=B("w_down_b", j))


        S.op("sp", lambda e: e.dma_start(out=cstf[:], in_=cst[:, :]), writes=[B("cstf")], dma=B("cstf"))
        S.op("sp", lambda e: e.dma_start(out=flagt[:], in_=flag[0:1, :].partition_broadcast(128)), writes=[B("flagt")], dma=B("flagt"))
        S.op("dve", lambda e: e.tensor_copy(out=trib[:], in_=cstf[:, 0:128]), reads=[B("cstf")], writes=[B("trib")])
        S.op("dve", lambda e: e.tensor_copy(out=identb[:], in_=cstf[:, 128:256]), reads=[B("cstf")], writes=[B("identb")])
        S.op("dve", lambda e: e.memset(onesb[:], 1.0), writes=[B("onesb")])
        S.op("dve", lambda e: e.memset(onesf[:], 1.0), writes=[B("onesf")])

        w_in_v = w_in_b.rearrange("(dc p) n -> p dc n", p=128)
        xTb_v = xTb.rearrange("(dc p) t -> p dc t", p=128)

        def dbg_dump(key, tile_ap, bufl):
            if key in dbg_out:
                S.op("sp", lambda e: e.dma_start(out=dbg_out[key][:, :], in_=tile_ap), reads=bufl, dma=B("dbg", key))

        ab = ExitStack()
        with ab:
            banks = [ab.enter_context(nc.psum_tensor("bank%d" % i, [128, 512], F32)) for i in range(8)]
            biasBb = sb("biasBb", [128, 8 * 5 * 128], BF16, ab)
            S.op("pool", lambda e: e.dma_start(out=biasBb[:], in_=biasB[:, :]), writes=[B("biasBb")], dma=B("biasBb"))
            register_casts()
            KT = sb("KT", [128, SEQ], BF16, ab)
            V = sb("V", [128, NB * 128], BF16, ab)
            QT = sb("QT", [128, NQ], BF16, ab)
            xcs = [sb("xc%d" % i, [128, 8 * 512], BF16, ab) for i in range(2)]
            wq = sb("wq", [128, 8 * 128], BF16, ab)
            wk = sb("wk", [128, 8 * 128], BF16, ab)
            wv = sb("wv", [128, 8 * 128], BF16, ab)
            wf = sb("wf", [128, 8 * 8], BF16, ab)
            pTs = [sb("pT%d" % i, [128, 512], BF16, ab) for i in range(4)]
            rec = sb("rec", [128, 512], F32, ab)
            Fz = sb("Fz", [128, 512], F32, ab)
            Lt = sb("Lt", [128, 512], F32, ab)
            TOT = sb("TOT", [128, 512], F32, ab)
            offs = sb("offs", [128, 512], F32, ab)
            cumL = sb("cumL", [128, 512], F32, ab)
            bfgt = sb("bfgt", [128, 512], F32, ab)
            maskAt = sb("maskAt", [128, NG * 64], F32, ab)
            maskBt = sb("maskBt", [128, NG * 8], F32, ab)
            biasAs = [sb("biasA%d" % i, [128, 128], F32, ab) for i in range(2)]

            S.op("sp", lambda e: e.dma_start(out=bfgt[:], in_=bfg[0:1, :].partition_broadcast(128)), writes=[B("bfgt")], dma=B("bfgt"))
            S.op("sp", lambda e: e.dma_start(out=maskAt[:], in_=maskA[0:1, :].partition_broadcast(128)), writes=[B("maskAt")], dma=B("maskAt"))
            S.op("sp", lambda e: e.dma_start(out=maskBt[:], in_=maskB[0:1, :].partition_broadcast(128)), writes=[B("maskBt")], dma=B("maskBt"))

            rot = {"S": 0, "P": 0, "proj": 0, "bA": 0}

            def proj_pass(kind, hp):
                if kind == "A":
                    qc, kc, vc = hp * 128, 512 + hp * 128, 1024 + hp * 128
                    c_start = 0
                else:
                    qc, kc, vc = 1544 + hp * 128, 2056 + hp * 128, 2568 + hp * 128
                    c_start = 6
                for (t, c0, nm) in ((wq, qc, "wq"), (wk, kc, "wk"), (wv, vc, "wv")):
                    S.op("sp", (lambda e, t=t, c0=c0: e.dma_start(out=t[:].rearrange("p (dc n) -> p dc n", dc=8),
                                                                  in_=w_in_v[:, :, c0:c0 + 128])),
                         reads=[B("w_in_b")], writes=[B(nm)], dma=B(nm))
                do_f = (kind == "A" and hp == 0)
                if do_f:
                    S.op("sp", lambda e: e.dma_start(out=wf[:].rearrange("p (dc n) -> p dc n", dc=8), in_=w_in_v[:, :, 1536:1544]),
                         reads=[B("w_in_b")], writes=[B("wf")], dma=B("wf"))
                for c in range(c_start, 16):
                    xc = xcs[c % 2]
                    Bxc = B("xc", c % 2)
                    S.op("sp", (lambda e, xc=xc, c=c: e.dma_start(out=xc[:].rearrange("p (dc t) -> p dc t", dc=8),
                                                                  in_=xTb_v[:, :, c * 512:(c + 1) * 512])),
                         reads=[B("xTb", c)], writes=[Bxc], dma=Bxc)
                    bi = rot["proj"] % 4
                    rot["proj"] += 1
                    ps = banks[bi]
                    for dc in range(8):
                        S.op("pe", (lambda e, ps=ps, xc=xc, dc=dc: e.matmul(ps[:, 0:512], lhsT=wk[:, dc * 128:(dc + 1) * 128],
                                                                         rhs=xc[:, dc * 512:(dc + 1) * 512], start=(dc == 0), stop=(dc == 7))),
                             reads=[B("wk"), Bxc], writes=[Bbank[bi]])
                    S.op("act", (lambda e, ps=ps, c=c: e.copy(out=KT[:, c * 512:(c + 1) * 512], in_=ps[:, 0:512])),
                         reads=[Bbank[bi]], writes=[B("KT", c)])
                    if c >= 7:
                        if c == 7:
                            xs, n, qo = 384, 128, 0
                        else:
                            xs, n, qo = 0, 512, 128 + (c - 8) * 512
                        bi = rot["proj"] % 4
                        rot["proj"] += 1
                        ps = banks[bi]
                        for dc in range(8):
                            S.op("pe", (lambda e, ps=ps, xc=xc, dc=dc, xs=xs, n=n: e.matmul(
                                ps[:, 0:n], lhsT=wq[:, dc * 128:(dc + 1) * 128], rhs=xc[:, dc * 512 + xs:dc * 512 + xs + n],
                                start=(dc == 0), stop=(dc == 7))), reads=[B("wq"), Bxc], writes=[Bbank[bi]])
                        S.op("act", (lambda e, ps=ps, n=n, qo=qo: e.copy(out=QT[:, qo:qo + n], in_=ps[:, 0:n])),
                             reads=[Bbank[bi]], writes=[B("QT", qo)])
                    bi = rot["proj"] % 4
                    rot["proj"] += 1
                    ps = banks[bi]
                    for blk in range(4):
                        for dc in range(8):
                            S.op("pe", (lambda e, ps=ps, xc=xc, dc=dc, blk=blk: e.matmul(
                                ps[:, blk * 128:(blk + 1) * 128], lhsT=xc[:, dc * 512 + blk * 128:dc * 512 + (blk + 1) * 128],
                                rhs=wv[:, dc * 128:(dc + 1) * 128], start=(dc == 0), stop=(dc == 7))),
                                reads=[B("wv"), Bxc], writes=[Bbank[bi]])
                    S.op("dve", (lambda e, ps=ps, c=c: e.tensor_copy(out=V[:, c * 512:(c + 1) * 512], in_=ps[:, 0:512])),
                         reads=[Bbank[bi]], writes=[B("V", c)])
                    if do_f:
                        bi = rot["proj"] % 4
                        rot["proj"] += 1
                        ps = banks[bi]
                        for blk in range(4):
                            for dc in range(8):
                                S.op("pe", (lambda e, ps=ps, xc=xc, dc=dc, blk=blk: e.matmul(
                                    ps[:, blk * 8:(blk + 1) * 8], lhsT=xc[:, dc * 512 + blk * 128:dc * 512 + (blk + 1) * 128],
                                    rhs=wf[:, dc * 8:(dc + 1) * 8], start=(dc == 0), stop=(dc == 7))),
                                    reads=[B("wf"), Bxc], writes=[Bbank[bi]])
                        S.op("dve", (lambda e, ps=ps, c=c: e.tensor_copy(out=Fz[:, c * 32:(c + 1) * 32], in_=ps[:, 0:32])),
                             reads=[Bbank[bi]], writes=[B("Fz")])

            def cum_compute():
                S.op("dve", lambda e: e.tensor_tensor(out=Fz[:], in0=Fz[:], in1=bfgt[:], op=ALU.add), reads=[B("Fz"), B("bfgt")], writes=[B("Fz")])
                S.op("dve", lambda e: e.tensor_scalar_max(out=Fz[:], in0=Fz[:], scalar1=-30.0), reads=[B("Fz")], writes=[B("Fz")])
                S.op("act", lambda e: e.activation(out=Lt[:], in_=Fz[:], func=AF.Exp, scale=-1.0), reads=[B("Fz")], writes=[B("Lt")])
                S.op("act", lambda e: e.activation(out=Lt[:], in_=Lt[:], func=AF.Ln, bias=1.0, scale=1.0), reads=[B("Lt")], writes=[B("Lt")])
                S.op("pe", lambda e: e.matmul(banks[5][:, 0:512], lhsT=utri, rhs=Lt[:], start=True, stop=True),
                     reads=[B("Lt"), B("cstf")], writes=[Bbank[5]])
                S.op("pe", lambda e: e.matmul(banks[6][:, 0:512], lhsT=onesf[:], rhs=Lt[:], start=True, stop=True),
                     reads=[B("Lt"), B("onesf")], writes=[Bbank[6]])
                S.op("dve", lambda e: e.tensor_copy(out=TOT[:], in_=banks[6][:, 0:512]), reads=[Bbank[6]], writes=[B("TOT")])
                S.op("dve", lambda e: e.memset(offs[:, 0:8], 0.0), writes=[B("offs")])
                for kb in range(1, NB):
                    S.op("dve", (lambda e, kb=kb: e.tensor_tensor(out=offs[:, kb * 8:(kb + 1) * 8], in0=offs[:, (kb - 1) * 8:kb * 8],
                                                                  in1=TOT[:, (kb - 1) * 8:kb * 8], op=ALU.add)),
                         reads=[B("offs"), B("TOT")], writes=[B("offs")])
                S.op("dve", lambda e: e.tensor_tensor(out=cumL[:], in0=banks[5][:, 0:512], in1=offs[:], op=ALU.add),
                     reads=[Bbank[5], B("offs")], writes=[B("cumL")])

            cumL_v = cumL[:].rearrange("p (kb h) -> p h kb", h=8)

            def attn_pass(kind, hp):
                oc = hp if kind == "A" else 4 + hp
                steps = []
                for g in range(NG):
                    kbs, N, qoff = ginfo(g)
                    nq = N // 128
                    nsteps = kbs + nq if kind == "A" else nq + 4
                    for st in range(nsteps):
                        steps.append((g, st, nsteps))
                ctx = {}

                def front(g, st, nsteps):
                    kbs, N, qoff = ginfo(g)
                    nq = N // 128
                    if st == 0 and kind == "A":
                        kref = kbs + 2 if nq == 4 else kbs
                        bA = biasAs[rot["bA"] % 2]
                        BbA = B("biasA", rot["bA"] % 2)
                        rot["bA"] += 1
                        ctx[g] = (bA, BbA)
                        for h in range(2):
                            H = 2 * hp + h
                            S.op("dve", (lambda e, bA=bA, h=h, H=H, nsteps=nsteps, kref=kref, g=g: e.scalar_tensor_tensor(
                                out=bA[:, h * 64:h * 64 + nsteps], in0=cumL_v[:, H, 0:nsteps],
                                scalar=offs[:, kref * 8 + H:kref * 8 + H + 1], in1=maskAt[:, g * 64:g * 64 + nsteps],
                                op0=ALU.subtract, op1=ALU.add)),
                                reads=[B("cumL"), B("offs"), B("maskAt")], writes=[BbA])
                    if kind == "A":
                        kb = st
                        i_lo, i_hi = max(0, kb - kbs), nq - 1
                    else:
                        kb = kbs - 4 + st
                        i_lo, i_hi = max(0, st - 4), min(nq - 1, st)
                    q_lo, q_hi = i_lo * 128, (i_hi + 1) * 128
                    Bk = B("KT", kb // 4)
                    info = []
                    for h in range(2):
                        H = 2 * hp + h
                        r0, r1 = h * 64, (h + 1) * 64
                        si = rot["S"] % 4
                        rot["S"] += 1
                        ps = banks[si]
                        extra = []
                        if kind == "A":
                            if kb >= kbs:
                                extra.append((i_lo, trib[:]))
                        else:
                            for i in range(i_lo, i_hi + 1):
                                o = i - (st - 4)
                                extra.append((i, biasBb[:, (H * 5 + o) * 128:(H * 5 + o + 1) * 128]))
                        S.op("pe", (lambda e, ps=ps, r0=r0, r1=r1, kb=kb, q_lo=q_lo, q_hi=q_hi, qoff=qoff, last=(not extra): e.matmul(
                            ps[:, q_lo:q_hi], lhsT=KT[r0:r1, kb * 128:(kb + 1) * 128], rhs=QT[r0:r1, qoff + q_lo:qoff + q_hi],
                            start=True, stop=last)), reads=[Bk, B("QT", qoff)], writes=[Bbank[si]])
                        for n_, (i, bt) in enumerate(extra):
                            S.op("pe", (lambda e, ps=ps, i=i, bt=bt, last=(n_ == len(extra) - 1): e.matmul(
                                ps[:, i * 128:(i + 1) * 128], lhsT=identb[:], rhs=bt, start=False, stop=last)),
                                reads=[B("identb"), B("trib"), B("biasBb")], writes=[Bbank[si]])
                        info.append((si, ps))
                    pis = []
                    for h in range(2):
                        si, ps = info[h]
                        pi = rot["P"] % 4
                        rot["P"] += 1
                        pT = pTs[pi]
                        if kind == "A":
                            bA, BbA = ctx[g]
                            bias_ap = bA[:, h * 64 + kb:h * 64 + kb + 1]
                            rd = [Bbank[si], BbA]
                        else:
                            bias_ap = maskBt[:, g * 8 + st:g * 8 + st + 1]
                            rd = [Bbank[si], B("maskBt")]
                        S.op("act", (lambda e, pT=pT, ps=ps, q_lo=q_lo, q_hi=q_hi, bias_ap=bias_ap: e.activation(
                            out=pT[:, q_lo:q_hi], in_=ps[:, q_lo:q_hi], func=AF.Exp, bias=bias_ap, scale=0.125)),
                            reads=rd, writes=[B("pT", pi)])
                        pis.append(pi)
                    return (kb, q_lo, q_hi, pis)

                def back(g, st, nsteps, fr):
                    kbs, N, qoff = ginfo(g)
                    kb, c0, c1, pis = fr
                    Bv = B("V", kb // 4)
                    bo, bd = (4, 5) if g % 2 == 0 else (6, 7)
                    po, pd = banks[bo], banks[bd]
                    first = (st == 0)
                    final = (st == nsteps - 1)
                    for h in range(2):
                        r0, r1 = h * 64, (h + 1) * 64
                        pT = pTs[pis[h]]
                        S.op("pe", (lambda e, r0=r0, r1=r1, kb=kb, h=h, pT=pT, po=po: e.matmul(
                            po[r0:r1, c0:c1], lhsT=V[:, kb * 128 + h * 64:kb * 128 + (h + 1) * 64], rhs=pT[:, c0:c1],
                            start=first, stop=final, skip_group_check=True)), reads=[Bv, B("pT", pis[h])], writes=[Bbank[bo]])
                    for h in range(2):
                        r0, r1 = h * 64, (h + 1) * 64
                        pT = pTs[pis[h]]
                        S.op("pe", (lambda e, r0=r0, r1=r1, pT=pT, pd=pd: e.matmul(
                            pd[r0:r1, c0:c1], lhsT=onesb[:, 0:64], rhs=pT[:, c0:c1],
                            start=first, stop=final, skip_group_check=True)), reads=[B("onesb"), B("pT", pis[h])], writes=[Bbank[bd]])
                    if final:
                        S.op("dve", (lambda e, N=N, pd=pd: e.reciprocal(out=rec[:, 0:N], in_=pd[:, 0:N])), reads=[Bbank[bd]], writes=[B("rec")])
                        S.op("dve", (lambda e, N=N, qoff=qoff, po=po: e.tensor_tensor(out=OT[oc][:, qoff:qoff + N], in0=po[:, 0:N],
                                                                                      in1=rec[:, 0:N], op=ALU.mult)),
                             reads=[Bbank[bo], B("rec")], writes=[B("OT", oc, g)])

                prev = None
                for i in range(len(steps) + 1):
                    cur = None
                    if i < len(steps):
                        cur = front(*steps[i])
                    if prev is not None:
                        back(*steps[i - 1], prev)
                    prev = cur

            for hp in range(STAGE["hpA"]):
                proj_pass("A", hp)
                if hp == 0:
                    cum_compute()
                    dbg_dump("cumL", cumL[:], [B("cumL")])
                attn_pass("A", hp)
            S.op("dve", lambda e: e.tensor_scalar(out=biasBb[:], in0=biasBb[:], scalar1=8.0, scalar2=None, op0=ALU.mult),
                 reads=[B("biasBb")], writes=[B("biasBb")])
            for hp in range(STAGE["hpB"]):
                proj_pass("B", hp)
                attn_pass("B", hp)
            for c in range(8):
                dbg_dump("OT%d" % c, OT[c][:], [B("OT", c, g) for g in range(NG)])
            S.flush(nc, es)

        pc = ExitStack()
        if not STAGE["C"]:
            S.op("sp", None, reads=[])
            return nc
        with pc:
            banks = [pc.enter_context(nc.psum_tensor("cbank%d" % i, [128, 512], F32)) for i in range(7)]
            bankT = pc.enter_context(nc.psum_tensor("bankT", [128, 1024], BF16))
            woutb = sb("woutb", [128, 8 * 1024], BF16, pc)
            lnt = sb("lnt", [128, 4 * 1024], F32, pc)
            cpt = sb("cpt", [128, 44 * 4], F32, pc)
            tails = sb("tails", [128, 44 * 2], F32, pc)
            x1g = sb("x1g", [128, 4 * 1024], F32, pc)
            xblk = [sb("xblk%d" % i, [128, 1024], F32, pc) for i in range(2)]
            obuf = [sb("obuf%d" % i, [128, 1024], F32, pc) for i in range(2)]
            x1b = [sb("x1b%d" % i, [128, 1024], BF16, pc) for i in range(2)]
            junkb = sb("junkb", [128, 1024], BF16, pc)
            x1T = sb("x1T", [128, 8 * 512], BF16, pc)
            hT = sb("hT", [128, NJ * 512], BF16, pc)
            av = [sb("av%d" % i, [128, 512], F32, pc) for i in range(2)]
            ag = [sb("ag%d" % i, [128, 512], F32, pc) for i in range(2)]
            wus = [sb("wu%d" % i, [128, 8 * 256], BF16, pc) for i in range(3)]
            wds = [sb("wd%d" % i, [128, 1024], BF16, pc) for i in range(6)]
            stt = sb("stt", [128, 2 * 32], F32, pc)
            epst = sb("epst", [128, 1], F32, pc)

            S.op("sp", lambda e: e.dma_start(out=woutb[:].rearrange("p (c n) -> p c n", c=8),
                                             in_=w_out_b.rearrange("(c p) n -> p c n", p=128)),
                 reads=[B("w_out_b")], writes=[B("woutb")], dma=B("woutb"))
            for i in range(4):
                S.op("sp", (lambda e, i=i: e.dma_start(out=lnt[:, i * 1024:(i + 1) * 1024], in_=lnp[i:i + 1, :].partition_broadcast(128))),
                     writes=[B("lnt")], dma=B("lnt"))
            S.op("sp", lambda e: e.dma_start(out=cpt[:], in_=convp[:, :]), writes=[B("cpt")], dma=B("cpt"))

            rc = {"x": 0, "wu": 0, "wd": 0, "u": 0, "o": 0, "up": 0, "m": 0, "ln": 0, "xb": 0}

            def layer_norm(items, gi):
                n = len(items)
                k = rc["ln"] % 2
                rc["ln"] += 1
                st = stt[:, k * 32:(k + 1) * 32]
                Bst = B("stt", k)
                S.op("dve", lambda e: e.memset(st[:, 0:8], 0.0), writes=[Bst])
                for i, (buf, Bbuf) in enumerate(items):
                    S.op("act", (lambda e, buf=buf, i=i: e.activation(out=junkb[:], in_=buf, func=AF.Identity, accum_out=st[:, i:i + 1])),
                         reads=[Bbuf, Bst], writes=[B("junkb"), Bst])
                    S.op("act", (lambda e, buf=buf, i=i: e.activation(out=junkb[:], in_=buf, func=AF.Square, accum_out=st[:, 4 + i:5 + i])),
                         reads=[Bbuf, Bst], writes=[B("junkb"), Bst])
                S.op("dve", lambda e: e.tensor_scalar(out=st[:, 8:8 + n], in0=st[:, 0:n], scalar1=1.0 / D, scalar2=None, op0=ALU.mult),
                     reads=[Bst], writes=[Bst])
                S.op("dve", lambda e: e.tensor_tensor(out=st[:, 12:12 + n], in0=st[:, 8:8 + n], in1=st[:, 8:8 + n], op=ALU.mult),
                     reads=[Bst], writes=[Bst])
                S.op("dve", lambda e: e.scalar_tensor_tensor(out=st[:, 16:16 + n], in0=st[:, 4:4 + n], scalar=1.0 / D, in1=st[:, 12:12 + n],
                                                             op0=ALU.mult, op1=ALU.subtract), reads=[Bst], writes=[Bst])
                S.op("act", lambda e: e.activation(out=st[:, 20:20 + n], in_=st[:, 16:16 + n], func=AF.Sqrt, bias=epst[:, 0:1], scale=1.0),
                     reads=[Bst, B("epst")], writes=[Bst])
                S.op("dve", lambda e: e.reciprocal(out=st[:, 24:24 + n], in_=st[:, 20:20 + n]), reads=[Bst], writes=[Bst])
                for i, (buf, Bbuf) in enumerate(items):
                    S.op("dve", (lambda e, buf=buf, i=i: e.scalar_tensor_tensor(out=buf, in0=buf, scalar=st[:, 8 + i:9 + i], in1=lnt[:, gi * 1024:(gi + 1) * 1024],
                                                                                 op0=ALU.subtract, op1=ALU.mult)), reads=[Bbuf, Bst, B("lnt")], writes=[Bbuf])
                    S.op("dve", (lambda e, buf=buf, i=i: e.scalar_tensor_tensor(out=buf, in0=buf, scalar=st[:, 24 + i:25 + i], in1=lnt[:, (gi + 1) * 1024:(gi + 2) * 1024],
                                                                                 op0=ALU.mult, op1=ALU.add)), reads=[Bbuf, Bst, B("lnt")], writes=[Bbuf])

            S.op("dve", lambda e: e.memset(epst[:], EPS), writes=[B("epst")])

            for g in range(NG):
                kbs, N, qoff = ginfo(g)
                nq = N // 128
                for tb in range(nq):
                    t0 = qoff + tb * 128
                    xi = rc["x"] % 2
                    rc["x"] += 1
                    xb = xblk[xi]
                    S.op("sp", (lambda e, xb=xb, t0=t0: e.dma_start(out=xb[:], in_=xo[t0:t0 + 128, :])), writes=[B("xblk", xi)], dma=B("xblk", xi))
                    for half in range(2):
                        bi = rc["m"] % 3
                        rc["m"] += 1
                        ps = banks[bi]
                        for c in range(8):
                            S.op("pe", (lambda e, ps=ps, c=c, t0=t0, half=half: e.matmul(
                                ps[:, 0:512], lhsT=OT[c][:, t0:t0 + 128], rhs=woutb[:, c * 1024 + half * 512:c * 1024 + (half + 1) * 512],
                                start=(c == 0), stop=(c == 7))), reads=[B("OT", c, g), B("woutb")], writes=[Bbank[bi]])
                        S.op("dve", (lambda e, ps=ps, xb=xb, half=half, tb=tb: e.scalar_tensor_tensor(
                            out=x1g[:, tb * 1024 + half * 512:tb * 1024 + (half + 1) * 512], in0=xb[:, half * 512:(half + 1) * 512], scalar=ALPHA,
                            in1=ps[:, 0:512], op0=ALU.mult, op1=ALU.add)), reads=[Bbank[bi], B("xblk", xi)], writes=[B("x1g", tb)])
                layer_norm([(x1g[:, tb * 1024:(tb + 1) * 1024], B("x1g", tb)) for tb in range(nq)], 0)
                for tb in range(nq):
                    x1 = x1g[:, tb * 1024:(tb + 1) * 1024]
                    xk = rc["xb"] % 2
                    rc["xb"] += 1
                    xbb = x1b[xk]
                    S.op("act", (lambda e, x1=x1, xbb=xbb: e.copy(out=xbb[:], in_=x1)), reads=[B("x1g", tb)], writes=[B("x1b", xk)])
                    for dc in range(8):
                        S.op("pe", (lambda e, dc=dc, xbb=xbb: e.transpose(bankT[:, dc * 128:(dc + 1) * 128], xbb[:, dc * 128:(dc + 1) * 128], identb[:])),
                             reads=[B("x1b", xk), B("identb")], writes=[BbankT])
                    S.op("act", (lambda e, N=N, tb=tb: e.copy(
                        out=x1T[:, 0:8 * N].rearrange("p (dc t) -> p dc t", dc=8)[:, :, tb * 128:(tb + 1) * 128],
                        in_=bankT[:, 0:1024].rearrange("p (dc t) -> p dc t", dc=8))), reads=[BbankT], writes=[B("x1T")])
                for j in range(NJ):
                    wi = rc["wu"] % 3
                    rc["wu"] += 1
                    wu = wus[wi]
                    S.op("sp", (lambda e, wu=wu, j=j: e.dma_start(out=wu[:].rearrange("p (dc n) -> p dc n", dc=8), in_=w_up_b[j])),
                         reads=[B("w_up_b", j)], writes=[B("wu", wi)], dma=B("wu", wi))
                    ui = rc["up"] % 2
                    rc["up"] += 1
                    bv, bg = 3 + 2 * ui, 4 + 2 * ui
                    for (bi, off) in ((bv, 0), (bg, 128)):
                        ps = banks[bi]
                        for dc in range(8):
                            S.op("pe", (lambda e, ps=ps, wu=wu, dc=dc, off=off, N=N: e.matmul(
                                ps[:, 0:N], lhsT=wu[:, dc * 256 + off:dc * 256 + off + 128], rhs=x1T[:, dc * N:(dc + 1) * N],
                                start=(dc == 0), stop=(dc == 7))), reads=[B("wu", wi), B("x1T")], writes=[Bbank[bi]])
                    if g == 0:
                        S.op("dve", (lambda e, j=j, bv=bv, N=N: e.tensor_scalar(out=tails[:, j * 2:j * 2 + 2], in0=banks[bv][:, N - 2:N],
                                                                                 scalar1=flagt[:, 0:1], scalar2=None, op0=ALU.mult)),
                             reads=[Bbank[bv], B("flagt")], writes=[B("tails", j)])
                        S.op("dve", (lambda e, j=j, bg=bg, N=N: e.tensor_scalar(out=tails[:, (NJ + j) * 2:(NJ + j) * 2 + 2], in0=banks[bg][:, N - 2:N],
                                                                                 scalar1=flagt[:, 0:1], scalar2=None, op0=ALU.mult)),
                             reads=[Bbank[bg], B("flagt")], writes=[B("tails", NJ + j)])
                        continue
                    k = rc["u"] % 2
                    rc["u"] += 1
                    for (bi, jj, acc, accn) in ((bv, j, av[k], "av"), (bg, NJ + j, ag[k], "ag")):
                        Bacc = B(accn, k)
                        ps = banks[bi]
                        w0 = cpt[:, jj * 4:jj * 4 + 1]
                        w1 = cpt[:, jj * 4 + 1:jj * 4 + 2]
                        w2 = cpt[:, jj * 4 + 2:jj * 4 + 3]
                        cb_ = cpt[:, jj * 4 + 3:jj * 4 + 4]
                        tl = tails[:, jj * 2:jj * 2 + 2]
                        Bt = B("tails", jj)
                        S.op("act", (lambda e, acc=acc, ps=ps, w2=w2, cb_=cb_: e.activation(out=acc[:], in_=ps[:, 0:512], func=AF.Identity,
                                                                                          bias=cb_, scale=w2)),
                             reads=[Bbank[bi], B("cpt")], writes=[Bacc])
                        S.op("dve", (lambda e, acc=acc, ps=ps, w1=w1: e.scalar_tensor_tensor(out=acc[:, 1:512], in0=ps[:, 0:511], scalar=w1,
                                                                                           in1=acc[:, 1:512], op0=ALU.mult, op1=ALU.add)),
                             reads=[Bbank[bi], B("cpt"), Bacc], writes=[Bacc])
                        S.op("dve", (lambda e, acc=acc, ps=ps, w0=w0: e.scalar_tensor_tensor(out=acc[:, 2:512], in0=ps[:, 0:510], scalar=w0,
                                                                                           in1=acc[:, 2:512], op0=ALU.mult, op1=ALU.add)),
                             reads=[Bbank[bi], B("cpt"), Bacc], writes=[Bacc])
                        S.op("dve", (lambda e, acc=acc, tl=tl, w1=w1: e.scalar_tensor_tensor(out=acc[:, 0:1], in0=tl[:, 1:2], scalar=w1,
                                                                                           in1=acc[:, 0:1], op0=ALU.mult, op1=ALU.add)),
                             reads=[Bt, B("cpt"), Bacc], writes=[Bacc])
                        S.op("dve", (lambda e, acc=acc, tl=tl, w0=w0: e.scalar_tensor_tensor(out=acc[:, 0:2], in0=tl[:, 0:2], scalar=w0,
                                                                                           in1=acc[:, 0:2], op0=ALU.mult, op1=ALU.add)),
                             reads=[Bt, B("cpt"), Bacc], writes=[Bacc])
                        S.op("dve", (lambda e, tl=tl, ps=ps: e.tensor_copy(out=tl, in_=ps[:, 510:512])), reads=[Bbank[bi]], writes=[Bt])
                    S.op("act", (lambda e, k=k: e.activation(out=ag[k][:], in_=ag[k][:], func=AF.Gelu_apprx_tanh)),
                         reads=[B("ag", k)], writes=[B("ag", k)])
                    S.op("dve", (lambda e, k=k, j=j: e.tensor_tensor(out=hT[:, j * 512:(j + 1) * 512], in0=av[k][:], in1=ag[k][:], op=ALU.mult)),
                         reads=[B("av", k), B("ag", k)], writes=[B("hT", j)])
                if g == 0:
                    continue
                for hh in range(2):
                    for j in range(NJ):
                        wi = rc["wd"] % 6
                        rc["wd"] += 1
                        wd = wds[wi]
                        S.op("sp", (lambda e, wd=wd, j=j: e.dma_start(out=wd[:], in_=w_down_b[j * 128:(j + 1) * 128, :])),
                             reads=[B("w_down_b", j)], writes=[B("wd", wi)], dma=B("wd", wi))
                        for tbl in range(2):
                            tb = 2 * hh + tbl
                            for half in range(2):
                                bi = 3 + tbl * 2 + half
                                S.op("pe", (lambda e, bi=bi, wd=wd, j=j, tb=tb, half=half: e.matmul(
                                    banks[bi][:, 0:512], lhsT=hT[:, j * 512 + tb * 128:j * 512 + (tb + 1) * 128],
                                    rhs=wd[:, half * 512:(half + 1) * 512], start=(j == 0), stop=(j == NJ - 1))),
                                    reads=[B("hT", j), B("wd", wi)], writes=[Bbank[bi]])
                    obs = []
                    for tbl in range(2):
                        tb = 2 * hh + tbl
                        oi = rc["o"] % 2
                        rc["o"] += 1
                        ob = obuf[oi]
                        for half in range(2):
                            bi = 3 + tbl * 2 + half
                            S.op("dve", (lambda e, bi=bi, tb=tb, half=half, ob=ob: e.scalar_tensor_tensor(
                                out=ob[:, half * 512:(half + 1) * 512], in0=x1g[:, tb * 1024 + half * 512:tb * 1024 + (half + 1) * 512],
                                scalar=ALPHA, in1=banks[bi][:, 0:512], op0=ALU.mult, op1=ALU.add)),
                                reads=[Bbank[bi], B("x1g", tb)], writes=[B("obuf", oi)])
                        obs.append((ob, oi, tb))
                    layer_norm([(ob[:], B("obuf", oi)) for (ob, oi, tb) in obs], 2)
                    for (ob, oi, tb) in obs:
                        r0 = (g - 1) * 512 + tb * 128
                        S.op("pool", (lambda e, ob=ob, r0=r0: e.dma_start(out=out[r0:r0 + 128, :], in_=ob[:])),
                             reads=[B("obuf", oi)], dma=B("obuf_st", oi))
            fin = [B("obuf_st", 0), B("obuf_st", 1)] + [B("dbg", k) for k in dbg_out]
            for b in fin:
                if b.cnt:
                    b.w = ("d", b, b.cnt)
            S.op("sp", None, reads=[b for b in fin if b.cnt])
            S.flush(nc, es)
    return nc


def host_inputs(x, w_in, b_forget, rel_bias, w_out, ln1_g, ln1_b, w_up, conv_w, conv_b, w_down, ln2_g, ln2_b):
    f = np.float32
    x = np.asarray(x, f)
    w_in0, w_out0, w_up0, w_down0 = (np.ascontiguousarray(np.asarray(a, f)[0]) for a in (w_in, w_out, w_up, w_down))
    lnp = np.ascontiguousarray(np.stack([np.asarray(a, f)[0] for a in (ln1_g, ln1_b, ln2_g, ln2_b)]))
    cw = np.asarray(conv_w, f)[0]
    cb = np.asarray(conv_b, f)[0]
    convp = np.zeros((128, 44, 4), f)
    for j in range(44):
        base = j * 128 if j < NJ else DFF + (j - NJ) * 128
        convp[:, j, 0:3] = cw[:, base:base + 128].T
        convp[:, j, 3] = cb[base:base + 128]
    convp = np.ascontiguousarray(convp.reshape(128, 176))
    bfgv = np.ascontiguousarray(np.tile(np.asarray(b_forget, f)[0], NB)[None, :])
    rb = np.asarray(rel_bias, f)[0]
    kk = np.arange(128)[:, None]
    qq = np.arange(128)[None, :]
    tiles = np.zeros((128, 8, 5, 128), f)
    for o in range(5):
        idx = np.clip(qq - kk + 128 * o, -128, 128) + 128
        t = rb[:, idx]
        t = np.transpose(t, (1, 0, 2)).copy()
        if o == 0:
            t[:, :, :][np.broadcast_to(((kk >= 64) & (qq < 64))[:, None, :], t.shape)] = NEG
        if o == 4:
            t[np.broadcast_to(((kk < 64) & (qq >= 64))[:, None, :], t.shape)] = NEG
        tiles[:, :, o, :] = t
    biasB = np.ascontiguousarray(tiles.reshape(128, 8 * 5 * 128))
    tri = np.where(kk <= qq, 0.0, NEG).astype(f)
    ident = np.eye(128, dtype=f)
    utri = (kk <= qq).astype(f)
    cst = np.ascontiguousarray(np.concatenate([tri, ident, utri], axis=1))
    maps = []
    for core in range(8):
        b, hi = core // 2, core % 2
        own = x[b, hi * HALF:(hi + 1) * HALF]
        prev = x[b, 0:HALF] if hi else own
        store = np.concatenate([prev, own], axis=0)
        xTm = np.ascontiguousarray(store.T)
        xom = np.ascontiguousarray(store[31 * 128:])
        mA = np.zeros((NG, 64), f)
        mB = np.zeros((NG, 8), f)
        if not hi:
            for g in range(NG):
                kbs, N, _ = ginfo(g)
                lim = 31 if g == 0 else 32
                mA[g, :lim] = NEG
                for st in range(8):
                    if kbs - 4 + st < lim:
                        mB[g, st] = NEG
        maps.append({"xT": xTm, "xo": xom, "w_in": w_in0, "w_out": w_out0, "w_up": w_up0, "w_down": w_down0,
                     "lnp": lnp, "convp": convp, "bfg": bfgv, "biasB": biasB,
                     "maskA": np.ascontiguousarray(mA.reshape(1, -1)), "maskB": np.ascontiguousarray(mB.reshape(1, -1)),
                     "flag": np.full((1, 1), float(hi), f), "cst": cst})
    return maps


_NC = None


def kernel(**inputs):
    global _NC
    maps = host_inputs(**inputs)
    if _NC is None:
        _NC = build_nc()
    res = run_bass_kernel_spmd(_NC, maps, core_ids=list(range(8)))
    outp = np.empty((4, SEQ, D), np.float32)
    for core in range(8):
        b, hi = core // 2, core % 2
        outp[b, hi * HALF:(hi + 1) * HALF] = np.asarray(res.results[core]["out"], np.float32)
    kernel.last_results = res
    return outp
```

```python
import numpy as np
from contextlib import ExitStack
import concourse.bass as bass
import concourse.mybir as mybir
from concourse.bass_utils import run_bass_kernel_spmd

F32, BF16 = mybir.dt.float32, mybir.dt.bfloat16
AF = mybir.ActivationFunctionType
ALU = mybir.AluOpType

D = 1024
SEQ = 8192
HALF = 4096
NB = 64
NG = 9
NQ = 4224
DFF = 2816
NJ = 22
PC = 3080
ALPHA = float(2.0 ** 0.25)
EPS = 1e-5
NEG = -30000.0
DEBUG = {}
STAGE = {"hpA": 4, "hpB": 4, "C": True}


def ginfo(g):
    if g == 0:
        return 31, 128, 0
    return 32 + 4 * (g - 1), 512, 128 + 512 * (g - 1)


class Buf:
    __slots__ = ("name", "w", "r", "sem", "cnt")

    def __init__(self, name):
        self.name = name
        self.w = None
        self.r = {}
        self.sem = None
        self.cnt = 0


class Op:
    __slots__ = ("fn", "waits", "signal", "dma")

    def __init__(self, fn, waits, dma):
        self.fn = fn
        self.waits = waits
        self.signal = False
        self.dma = dma


class Sched:
    ENG = ("pe", "act", "dve", "pool", "sp")

    def __init__(self):
        self.ops = {e: [] for e in self.ENG}
        self.bufs = {}
        self.dmabufs = []

    def B(self, *key):
        b = self.bufs.get(key)
        if b is None:
            b = Buf(key)
            self.bufs[key] = b
        return b

    def op(self, eng, fn, reads=(), writes=(), dma=None):
        deps = set()
        for b in reads:
            if b.w is not None:
                deps.add(b.w)
        for b in writes:
            if b.w is not None:
                deps.add(b.w)
            deps.update(b.r.values())
        seq = len(self.ops[eng])
        if dma is not None:
            if dma.cnt == 0:
                self.dmabufs.append(dma)
            dma.cnt += 16
            ev = ("d", dma, dma.cnt)
            rkey = ("d", id(dma))
        else:
            ev = ("e", eng, seq)
            rkey = eng
        waits = []
        for d in deps:
            if d[0] == "e" and d[1] == eng and (eng == "pe" or dma is not None and False):
                continue
            if d[0] == "e":
                fr = getattr(self, "frozen", None)
                if fr is not None and d[2] <= fr[d[1]]:
                    if not (self.ops[d[1]][d[2]].signal and self.ops[d[1]][d[2]].fn is not None):
                        continue
                else:
                    self.ops[d[1]][d[2]].signal = True
            waits.append(d)
        for b in reads:
            b.r[rkey] = ev
        for b in writes:
            b.w = ev
            b.r = {}
        self.ops[eng].append(Op(fn, waits, dma))
        return ev

    def flush(self, nc, es):
        if not hasattr(self, "sems"):
            self.sems = {e: es.enter_context(nc.semaphore("sem_" + e)) for e in self.ENG}
            self.done = {e: 0 for e in self.ENG}
            self.frozen = {e: -1 for e in self.ENG}
            self.nds = 0
        evs = []
        for e in self.ENG:
            n = len(self.ops[e])
            if n:
                j = n - 1
                while j >= 0 and (self.ops[e][j].fn is None or self.ops[e][j].dma is not None):
                    j -= 1
                if j >= 0:
                    self.ops[e][j].signal = True
                    evs.append(("e", e, j))
        for b in self.dmabufs:
            evs.append(("d", b, b.cnt))
        for e in self.ENG:
            self.ops[e].append(Op(None, list(evs), None))
        for b in self.dmabufs:
            if b.sem is None:
                b.sem = es.enter_context(nc.semaphore("dsem%d" % self.nds))
                self.nds += 1
        cum = {}
        for e in self.ENG:
            c = 0
            arr = []
            for o in self.ops[e]:
                if o.signal and o.dma is None and o.fn is not None:
                    c += 1
                arr.append(c)
            cum[e] = arr
        sems = self.sems
        sched = self
        lo = dict(self.done)
        if not hasattr(self, "waited"):
            self.waited = {e: {} for e in self.ENG}

        def replay(e, eobj):
            waited = sched.waited[e]
            for o in sched.ops[e][lo[e]:]:
                for d in o.waits:
                    if d[0] == "e":
                        sem, val = sems[d[1]], cum[d[1]][d[2]]
                    else:
                        sem, val = d[1].sem, d[2]
                    k = id(sem)
                    if waited.get(k, 0) >= val:
                        continue
                    waited[k] = val
                    eobj.wait_ge(sem, val)
                if o.fn is None:
                    continue
                ins = o.fn(eobj)
                if o.dma is not None:
                    ins.then_inc(o.dma.sem, 16)
                elif o.signal:
                    ins.then_inc(sems[e], 1)

        with nc.Block() as block:
            @block.tensor
            def _(t):
                replay("pe", t)

            @block.scalar
            def _(t):
                replay("act", t)

            @block.vector
            def _(t):
                replay("dve", t)

            @block.gpsimd
            def _(t):
                replay("pool", t)

            @block.sync
            def _(t):
                replay("sp", t)
        for e in self.ENG:
            self.done[e] = len(self.ops[e])
            self.frozen[e] = len(self.ops[e]) - 1


def build_nc():
    nc = bass.Bass("TRN2", target_bir_lowering=False)
    S = Sched()
    B = S.B

    def din(name, shape):
        return nc.dram_tensor(name, shape, F32, kind="ExternalInput").ap()

    xT = din("xT", [D, SEQ])
    xo = din("xo", [NQ, D])
    w_in = din("w_in", [D, PC])
    w_out = din("w_out", [D, D])
    w_up = din("w_up", [D, 2 * DFF])
    w_down = din("w_down", [DFF, D])
    lnp = din("lnp", [4, D])
    convp = din("convp", [128, 44 * 4])
    bfg = din("bfg", [1, 512])
    biasB = din("biasB", [128, 8 * 5 * 128])
    maskA = din("maskA", [1, NG * 64])
    maskB = din("maskB", [1, NG * 8])
    flag = din("flag", [1, 1])
    cst = din("cst", [128, 3 * 128])
    out = nc.dram_tensor("out", [HALF, D], F32, kind="ExternalOutput").ap()

    xTb = nc.dram_tensor("xTb", [D, SEQ], BF16).ap()
    w_in_b = nc.dram_tensor("w_in_b", [D, PC], BF16).ap()
    w_out_b = nc.dram_tensor("w_out_b", [D, D], BF16).ap()
    w_up_b = nc.dram_tensor("w_up_b", [NJ, 128, 8, 256], BF16).ap()
    w_down_b = nc.dram_tensor("w_down_b", [DFF, D], BF16).ap()

    dbg_out = {}
    for k, shp in DEBUG.items():
        dbg_out[k] = nc.dram_tensor("dbg_" + k, list(shp[0]), shp[1], kind="ExternalOutput").ap()

    es = ExitStack()
    with es:
        def sb(name, shape, dt, stack=es):
            return stack.enter_context(nc.sbuf_tensor(name, shape, dt))

        Bbank = [B("bank", i) for i in range(8)]
        BbankT = B("bankT")

        OT = [sb("OT%d" % c, [128, NQ], BF16) for c in range(8)]
        cstf = sb("cstf", [128, 384], F32)
        trib = sb("trib", [128, 128], BF16)
        identb = sb("identb", [128, 128], BF16)
        onesb = sb("onesb", [128, 128], BF16)
        onesf = sb("onesf", [128, 128], F32)
        flagt = sb("flagt", [128, 1], F32)
        utri = cstf[:, 256:384]

        def register_casts():
            for r in range(8):
                S.op("pool", (lambda e, r=r: e.dma_start(out=w_in_b[r * 128:(r + 1) * 128, :], in_=w_in[r * 128:(r + 1) * 128, :])),
                     writes=[B("w_in_b")], dma=B("w_in_b"))
            for c in range(16):
                S.op("pool", (lambda e, c=c: e.dma_start(out=xTb[:, c * 512:(c + 1) * 512], in_=xT[:, c * 512:(c + 1) * 512])),
                     writes=[B("xTb", c)], dma=B("xTb", c))
            for r in range(8):
                S.op("pool", (lambda e, r=r: e.dma_start(out=w_out_b[r * 128:(r + 1) * 128, :], in_=w_out[r * 128:(r + 1) * 128, :])),
                     writes=[B("w_out_b")], dma=B("w_out_b"))
            w_up_v = w_up.rearrange("(dc p) n -> p dc n", p=128)
            for j in range(NJ):
                for part in range(2):
                    c0 = part * DFF + j * 128
                    S.op("pool", (lambda e, j=j, part=part, c0=c0: e.dma_start(
                        out=w_up_b[j, :, :, part * 128:(part + 1) * 128], in_=w_up_v[:, :, c0:c0 + 128])),
                        writes=[B("w_up_b", j)], dma=B("w_up_b", j))
            for j in range(NJ):
                S.op("pool", (lambda e, j=j: e.dma_start(out=w_down_b[j * 128:(j + 1) * 128, :], in_=w_down[j * 128:(j + 1) * 128, :])),
                     writes=[B("w_down_b", j)], dma=B("w_down_b", j))


        S.op("sp", lambda e: e.dma_start(out=cstf[:], in_=cst[:, :]), writes=[B("cstf")], dma=B("cstf"))
        S.op("sp", lambda e: e.dma_start(out=flagt[:], in_=flag[0:1, :].partition_broadcast(128)), writes=[B("flagt")], dma=B("flagt"))
        S.op("dve", lambda e: e.tensor_copy(out=trib[:], in_=cstf[:, 0:128]), reads=[B("cstf")], writes=[B("trib")])
        S.op("dve", lambda e: e.tensor_copy(out=identb[:], in_=cstf[:, 128:256]), reads=[B("cstf")], writes=[B("identb")])
        S.op("dve", lambda e: e.memset(onesb[:], 1.0), writes=[B("onesb")])
        S.op("dve", lambda e: e.memset(onesf[:], 1.0), writes=[B("onesf")])

        w_in_v = w_in_b.rearrange("(dc p) n -> p dc n", p=128)
        xTb_v = xTb.rearrange("(dc p) t -> p dc t", p=128)

        def dbg_dump(key, tile_ap, bufl):
            if key in dbg_out:
                S.op("sp", lambda e: e.dma_start(out=dbg_out[key][:, :], in_=tile_ap), reads=bufl, dma=B("dbg", key))

        ab = ExitStack()
        with ab:
            banks = [ab.enter_context(nc.psum_tensor("bank%d" % i, [128, 512], F32)) for i in range(8)]
            biasBb = sb("biasBb", [128, 8 * 5 * 128], BF16, ab)
            S.op("pool", lambda e: e.dma_start(out=biasBb[:], in_=biasB[:, :]), writes=[B("biasBb")], dma=B("biasBb"))
            register_casts()
            KT = sb("KT", [128, SEQ], BF16, ab)
            V = sb("V", [128, NB * 128], BF16, ab)
            QT = sb("QT", [128, NQ], BF16, ab)
            xcs = [sb("xc%d" % i, [128, 8 * 512], BF16, ab) for i in range(2)]
            wq = sb("wq", [128, 8 * 128], BF16, ab)
            wk = sb("wk", [128, 8 * 128], BF16, ab)
            wv = sb("wv", [128, 8 * 128], BF16, ab)
            wf = sb("wf", [128, 8 * 8], BF16, ab)
            pTs = [sb("pT%d" % i, [128, 512], BF16, ab) for i in range(4)]
            rec = sb("rec", [128, 512], F32, ab)
            Fz = sb("Fz", [128, 512], F32, ab)
            Lt = sb("Lt", [128, 512], F32, ab)
            TOT = sb("TOT", [128, 512], F32, ab)
            offs = sb("offs", [128, 512], F32, ab)
            cumL = sb("cumL", [128, 512], F32, ab)
            bfgt = sb("bfgt", [128, 512], F32, ab)
            maskAt = sb("maskAt", [128, NG * 64], F32, ab)
            maskBt = sb("maskBt", [128, NG * 8], F32, ab)
            biasAs = [sb("biasA%d" % i, [128, 128], F32, ab) for i in range(2)]

            S.op("sp", lambda e: e.dma_start(out=bfgt[:], in_=bfg[0:1, :].partition_broadcast(128)), writes=[B("bfgt")], dma=B("bfgt"))
            S.op("sp", lambda e: e.dma_start(out=maskAt[:], in_=maskA[0:1, :].partition_broadcast(128)), writes=[B("maskAt")], dma=B("maskAt"))
            S.op("sp", lambda e: e.dma_start(out=maskBt[:], in_=maskB[0:1, :].partition_broadcast(128)), writes=[B("maskBt")], dma=B("maskBt"))

            rot = {"S": 0, "P": 0, "proj": 0, "bA": 0}

            def proj_pass(kind, hp):
                if kind == "A":
                    qc, kc, vc = hp * 128, 512 + hp * 128, 1024 + hp * 128
                    c_start = 0
                else:
                    qc, kc, vc = 1544 + hp * 128, 2056 + hp * 128, 2568 + hp * 128
                    c_start = 6
                for (t, c0, nm) in ((wq, qc, "wq"), (wk, kc, "wk"), (wv, vc, "wv")):
                    S.op("sp", (lambda e, t=t, c0=c0: e.dma_start(out=t[:].rearrange("p (dc n) -> p dc n", dc=8),
                                                                  in_=w_in_v[:, :, c0:c0 + 128])),
                         reads=[B("w_in_b")], writes=[B(nm)], dma=B(nm))
                do_f = (kind == "A" and hp == 0)
                if do_f:
                    S.op("sp", lambda e: e.dma_start(out=wf[:].rearrange("p (dc n) -> p dc n", dc=8), in_=w_in_v[:, :, 1536:1544]),
                         reads=[B("w_in_b")], writes=[B("wf")], dma=B("wf"))
                for c in range(c_start, 16):
                    xc = xcs[c % 2]
                    Bxc = B("xc", c % 2)
                    S.op("sp", (lambda e, xc=xc, c=c: e.dma_start(out=xc[:].rearrange("p (dc t) -> p dc t", dc=8),
                                                                  in_=xTb_v[:, :, c * 512:(c + 1) * 512])),
                         reads=[B("xTb", c)], writes=[Bxc], dma=Bxc)
                    bi = rot["proj"] % 4
                    rot["proj"] += 1
                    ps = banks[bi]
                    for dc in range(8):
                        S.op("pe", (lambda e, ps=ps, xc=xc, dc=dc: e.matmul(ps[:, 0:512], lhsT=wk[:, dc * 128:(dc + 1) * 128],
                                                                         rhs=xc[:, dc * 512:(dc + 1) * 512], start=(dc == 0), stop=(dc == 7))),
                             reads=[B("wk"), Bxc], writes=[Bbank[bi]])
                    S.op("act", (lambda e, ps=ps, c=c: e.copy(out=KT[:, c * 512:(c + 1) * 512], in_=ps[:, 0:512])),
                         reads=[Bbank[bi]], writes=[B("KT", c)])
                    if c >= 7:
                        if c == 7:
                            xs, n, qo = 384, 128, 0
                        else:
                            xs, n, qo = 0, 512, 128 + (c - 8) * 512
                        bi = rot["proj"] % 4
                        rot["proj"] += 1
                        ps = banks[bi]
                        for dc in range(8):
                            S.op("pe", (lambda e, ps=ps, xc=xc, dc=dc, xs=xs, n=n: e.matmul(
                                ps[:, 0:n], lhsT=wq[:, dc * 128:(dc + 1) * 128], rhs=xc[:, dc * 512 + xs:dc * 512 + xs + n],
                                start=(dc == 0), stop=(dc == 7))), reads=[B("wq"), Bxc], writes=[Bbank[bi]])
                        S.op("act", (lambda e, ps=ps, n=n, qo=qo: e.copy(out=QT[:, qo:qo + n], in_=ps[:, 0:n])),
                             reads=[Bbank[bi]], writes=[B("QT", qo)])
                    bi = rot["proj"] % 4
                    rot["proj"] += 1
                    ps = banks[bi]
                    for blk in range(4):
                        for dc in range(8):
                            S.op("pe", (lambda e, ps=ps, xc=xc, dc=dc, blk=blk: e.matmul(
                                ps[:, blk * 128:(blk + 1) * 128], lhsT=xc[:, dc * 512 + blk * 128:dc * 512 + (blk + 1) * 128],
                                rhs=wv[:, dc * 128:(dc + 1) * 128], start=(dc == 0), stop=(dc == 7))),
                                reads=[B("wv"), Bxc], writes=[Bbank[bi]])
                    S.op("dve", (lambda e, ps=ps, c=c: e.tensor_copy(out=V[:, c * 512:(c + 1) * 512], in_=ps[:, 0:512])),
                         reads=[Bbank[bi]], writes=[B("V", c)])
                    if do_f:
                        bi = rot["proj"] % 4
                        rot["proj"] += 1
                        ps = banks[bi]
                        for blk in range(4):
                            for dc in range(8):
                                S.op("pe", (lambda e, ps=ps, xc=xc, dc=dc, blk=blk: e.matmul(
                                    ps[:, blk * 8:(blk + 1) * 8], lhsT=xc[:, dc * 512 + blk * 128:dc * 512 + (blk + 1) * 128],
                                    rhs=wf[:, dc * 8:(dc + 1) * 8], start=(dc == 0), stop=(dc == 7))),
                                    reads=[B("wf"), Bxc], writes=[Bbank[bi]])
                        S.op("dve", (lambda e, ps=ps, c=c: e.tensor_copy(out=Fz[:, c * 32:(c + 1) * 32], in_=ps[:, 0:32])),
                             reads=[Bbank[bi]], writes=[B("Fz")])

            def cum_compute():
                S.op("dve", lambda e: e.tensor_tensor(out=Fz[:], in0=Fz[:], in1=bfgt[:], op=ALU.add), reads=[B("Fz"), B("bfgt")], writes=[B("Fz")])
                S.op("dve", lambda e: e.tensor_scalar_max(out=Fz[:], in0=Fz[:], scalar1=-30.0), reads=[B("Fz")], writes=[B("Fz")])
                S.op("act", lambda e: e.activation(out=Lt[:], in_=Fz[:], func=AF.Exp, scale=-1.0), reads=[B("Fz")], writes=[B("Lt")])
                S.op("act", lambda e: e.activation(out=Lt[:], in_=Lt[:], func=AF.Ln, bias=1.0, scale=1.0), reads=[B("Lt")], writes=[B("Lt")])
                S.op("pe", lambda e: e.matmul(banks[5][:, 0:512], lhsT=utri, rhs=Lt[:], start=True, stop=True),
                     reads=[B("Lt"), B("cstf")], writes=[Bbank[5]])
                S.op("pe", lambda e: e.matmul(banks[6][:, 0:512], lhsT=onesf[:], rhs=Lt[:], start=True, stop=True),
                     reads=[B("Lt"), B("onesf")], writes=[Bbank[6]])
                S.op("dve", lambda e: e.tensor_copy(out=TOT[:], in_=banks[6][:, 0:512]), reads=[Bbank[6]], writes=[B("TOT")])
                S.op("dve", lambda e: e.memset(offs[:, 0:8], 0.0), writes=[B("offs")])
                for kb in range(1, NB):
                    S.op("dve", (lambda e, kb=kb: e.tensor_tensor(out=offs[:, kb * 8:(kb + 1) * 8], in0=offs[:, (kb - 1) * 8:kb * 8],
                                                                  in1=TOT[:, (kb - 1) * 8:kb * 8], op=ALU.add)),
                         reads=[B("offs"), B("TOT")], writes=[B("offs")])
                S.op("dve", lambda e: e.tensor_tensor(out=cumL[:], in0=banks[5][:, 0:512], in1=offs[:], op=ALU.add),
                     reads=[Bbank[5], B("offs")], writes=[B("cumL")])

            cumL_v = cumL[:].rearrange("p (kb h) -> p h kb", h=8)

            def attn_pass(kind, hp):
                oc = hp if kind == "A" else 4 + hp
                steps = []
                for g in range(NG):
                    kbs, N, qoff = ginfo(g)
                    nq = N // 128
                    nsteps = kbs + nq if kind == "A" else nq + 4
                    for st in range(nsteps):
                        steps.append((g, st, nsteps))
                ctx = {}

                def front(g, st, nsteps):
                    kbs, N, qoff = ginfo(g)
                    nq = N // 128
                    if st == 0 and kind == "A":
                        kref = kbs + 2 if nq == 4 else kbs
                        bA = biasAs[rot["bA"] % 2]
                        BbA = B("biasA", rot["bA"] % 2)
                        rot["bA"] += 1
                        ctx[g] = (bA, BbA)
                        for h in range(2):
                            H = 2 * hp + h
                            S.op("dve", (lambda e, bA=bA, h=h, H=H, nsteps=nsteps, kref=kref, g=g: e.scalar_tensor_tensor(
                                out=bA[:, h * 64:h * 64 + nsteps], in0=cumL_v[:, H, 0:nsteps],
                                scalar=offs[:, kref * 8 + H:kref * 8 + H + 1], in1=maskAt[:, g * 64:g * 64 + nsteps],
                                op0=ALU.subtract, op1=ALU.add)),
                                reads=[B("cumL"), B("offs"), B("maskAt")], writes=[BbA])
                    if kind == "A":
                        kb = st
                        i_lo, i_hi = max(0, kb - kbs), nq - 1
                    else:
                        kb = kbs - 4 + st
                        i_lo, i_hi = max(0, st - 4), min(nq - 1, st)
                    q_lo, q_hi = i_lo * 128, (i_hi + 1) * 128
                    Bk = B("KT", kb // 4)
                    info = []
                    for h in range(2):
                        H = 2 * hp + h
                        r0, r1 = h * 64, (h + 1) * 64
                        si = rot["S"] % 4
                        rot["S"] += 1
                        ps = banks[si]
                        extra = []
                        if kind == "A":
                            if kb >= kbs:
                                extra.append((i_lo, trib[:]))
                        else:
                            for i in range(i_lo, i_hi + 1):
                                o = i - (st - 4)
                                extra.append((i, biasBb[:, (H * 5 + o) * 128:(H * 5 + o + 1) * 128]))
                        S.op("pe", (lambda e, ps=ps, r0=r0, r1=r1, kb=kb, q_lo=q_lo, q_hi=q_hi, qoff=qoff, last=(not extra): e.matmul(
                            ps[:, q_lo:q_hi], lhsT=KT[r0:r1, kb * 128:(kb + 1) * 128], rhs=QT[r0:r1, qoff + q_lo:qoff + q_hi],
                            start=True, stop=last)), reads=[Bk, B("QT", qoff)], writes=[Bbank[si]])
                        for n_, (i, bt) in enumerate(extra):
                            S.op("pe", (lambda e, ps=ps, i=i, bt=bt, last=(n_ == len(extra) - 1): e.matmul(
                                ps[:, i * 128:(i + 1) * 128], lhsT=identb[:], rhs=bt, start=False, stop=last)),
                                reads=[B("identb"), B("trib"), B("biasBb")], writes=[Bbank[si]])
                        info.append((si, ps))
                    pis = []
                    for h in range(2):
                        si, ps = info[h]
                        pi = rot["P"] % 4
                        rot["P"] += 1
                        pT = pTs[pi]
                        if kind == "A":
                            bA, BbA = ctx[g]
                            bias_ap = bA[:, h * 64 + kb:h * 64 + kb + 1]
                            rd = [Bbank[si], BbA]
                        else:
                            bias_ap = maskBt[:, g * 8 + st:g * 8 + st + 1]
                            rd = [Bbank[si], B("maskBt")]
                        S.op("act", (lambda e, pT=pT, ps=ps, q_lo=q_lo, q_hi=q_hi, bias_ap=bias_ap: e.activation(
                            out=pT[:, q_lo:q_hi], in_=ps[:, q_lo:q_hi], func=AF.Exp, bias=bias_ap, scale=0.125)),
                            reads=rd, writes=[B("pT", pi)])
                        pis.append(pi)
                    return (kb, q_lo, q_hi, pis)

                def back(g, st, nsteps, fr):
                    kbs, N, qoff = ginfo(g)
                    kb, c0, c1, pis = fr
                    Bv = B("V", kb // 4)
                    bo, bd = (4, 5) if g % 2 == 0 else (6, 7)
                    po, pd = banks[bo], banks[bd]
                    first = (st == 0)
                    final = (st == nsteps - 1)
                    for h in range(2):
                        r0, r1 = h * 64, (h + 1) * 64
                        pT = pTs[pis[h]]
                        S.op("pe", (lambda e, r0=r0, r1=r1, kb=kb, h=h, pT=pT, po=po: e.matmul(
                            po[r0:r1, c0:c1], lhsT=V[:, kb * 128 + h * 64:kb * 128 + (h + 1) * 64], rhs=pT[:, c0:c1],
                            start=first, stop=final, skip_group_check=True)), reads=[Bv, B("pT", pis[h])], writes=[Bbank[bo]])
                    for h in range(2):
                        r0, r1 = h * 64, (h + 1) * 64
                        pT = pTs[pis[h]]
                        S.op("pe", (lambda e, r0=r0, r1=r1, pT=pT, pd=pd: e.matmul(
                            pd[r0:r1, c0:c1], lhsT=onesb[:, 0:64], rhs=pT[:, c0:c1],
                            start=first, stop=final, skip_group_check=True)), reads=[B("onesb"), B("pT", pis[h])], writes=[Bbank[bd]])
                    if final:
                        S.op("dve", (lambda e, N=N, pd=pd: e.reciprocal(out=rec[:, 0:N], in_=pd[:, 0:N])), reads=[Bbank[bd]], writes=[B("rec")])
                        S.op("dve", (lambda e, N=N, qoff=qoff, po=po: e.tensor_tensor(out=OT[oc][:, qoff:qoff + N], in0=po[:, 0:N],
                                                                                      in1=rec[:, 0:N], op=ALU.mult)),
                             reads=[Bbank[bo], B("rec")], writes=[B("OT", oc, g)])

                prev = None
                for i in range(len(steps) + 1):
                    cur = None
                    if i < len(steps):
                        cur = front(*steps[i])
                    if prev is not None:
                        back(*steps[i - 1], prev)
                    prev = cur

            for hp in range(STAGE["hpA"]):
                proj_pass("A", hp)
                if hp == 0:
                    cum_compute()
                    dbg_dump("cumL", cumL[:], [B("cumL")])
                attn_pass("A", hp)
            S.op("dve", lambda e: e.tensor_scalar(out=biasBb[:], in0=biasBb[:], scalar1=8.0, scalar2=None, op0=ALU.mult),
                 reads=[B("biasBb")], writes=[B("biasBb")])
            for hp in range(STAGE["hpB"]):
                proj_pass("B", hp)
                attn_pass("B", hp)
            for c in range(8):
                dbg_dump("OT%d" % c, OT[c][:], [B("OT", c, g) for g in range(NG)])
            S.flush(nc, es)

        pc = ExitStack()
        if not STAGE["C"]:
            S.op("sp", None, reads=[])
            return nc
        with pc:
            banks = [pc.enter_context(nc.psum_tensor("cbank%d" % i, [128, 512], F32)) for i in range(7)]
            bankT = pc.enter_context(nc.psum_tensor("bankT", [128, 1024], BF16))
            woutb = sb("woutb", [128, 8 * 1024], BF16, pc)
            lnt = sb("lnt", [128, 4 * 1024], F32, pc)
            cpt = sb("cpt", [128, 44 * 4], F32, pc)
            tailss = [sb("tails%d" % i, [128, 44 * 2], F32, pc) for i in range(2)]
            carry = sb("carry", [128, 44 * 2], F32, pc)
            x1gs = [sb("x1g%d" % i, [128, 4 * 1024], F32, pc) for i in range(2)]
            xblk = sb("xblk", [128, 1024], F32, pc)
            x1b = sb("x1b", [128, 1024], BF16, pc)
            junkb = sb("junkb", [128, 1024], BF16, pc)
            x1T = sb("x1T", [128, 8 * 512], BF16, pc)
            hT = sb("hT", [128, NJ * 512], BF16, pc)
            av = [sb("av%d" % i, [128, 512], F32, pc) for i in range(2)]
            ag = [sb("ag%d" % i, [128, 512], F32, pc) for i in range(2)]
            wus = [sb("wu%d" % i, [128, 8 * 256], BF16, pc) for i in range(3)]
            NWD = 5
            wds = [sb("wd%d" % i, [128, 1024], BF16, pc) for i in range(NWD)]
            stt = sb("stt", [128, 2 * 32], F32, pc)
            epst = sb("epst", [128, 1], F32, pc)

            S.op("sp", lambda e: e.dma_start(out=woutb[:].rearrange("p (c n) -> p c n", c=8),
                                             in_=w_out_b.rearrange("(c p) n -> p c n", p=128)),
                 reads=[B("w_out_b")], writes=[B("woutb")], dma=B("woutb"))
            for i in range(4):
                S.op("sp", (lambda e, i=i: e.dma_start(out=lnt[:, i * 1024:(i + 1) * 1024], in_=lnp[i:i + 1, :].partition_broadcast(128))),
                     writes=[B("lnt")], dma=B("lnt"))
            S.op("sp", lambda e: e.dma_start(out=cpt[:], in_=convp[:, :]), writes=[B("cpt")], dma=B("cpt"))

            rc = {"x": 0, "wu": 0, "wd": 0, "u": 0, "o": 0, "up": 0, "m": 0, "ln": 0, "xb": 0}

            def layer_norm(items, gi):
                n = len(items)
                k = rc["ln"] % 2
                rc["ln"] += 1
                st = stt[:, k * 32:(k + 1) * 32]
                Bst = B("stt", k)
                S.op("dve", lambda e: e.memset(st[:, 0:8], 0.0), writes=[Bst])
                for i, (buf, Bbuf) in enumerate(items):
                    S.op("act", (lambda e, buf=buf, i=i: e.activation(out=junkb[:], in_=buf, func=AF.Identity, accum_out=st[:, i:i + 1])),
                         reads=[Bbuf, Bst], writes=[B("junkb"), Bst])
                    S.op("act", (lambda e, buf=buf, i=i: e.activation(out=junkb[:], in_=buf, func=AF.Square, accum_out=st[:, 4 + i:5 + i])),
                         reads=[Bbuf, Bst], writes=[B("junkb"), Bst])
                S.op("dve", lambda e: e.tensor_scalar(out=st[:, 8:8 + n], in0=st[:, 0:n], scalar1=1.0 / D, scalar2=None, op0=ALU.mult),
                     reads=[Bst], writes=[Bst])
                S.op("dve", lambda e: e.tensor_tensor(out=st[:, 12:12 + n], in0=st[:, 8:8 + n], in1=st[:, 8:8 + n], op=ALU.mult),
                     reads=[Bst], writes=[Bst])
                S.op("dve", lambda e: e.scalar_tensor_tensor(out=st[:, 16:16 + n], in0=st[:, 4:4 + n], scalar=1.0 / D, in1=st[:, 12:12 + n],
                                                             op0=ALU.mult, op1=ALU.subtract), reads=[Bst], writes=[Bst])
                S.op("act", lambda e: e.activation(out=st[:, 20:20 + n], in_=st[:, 16:16 + n], func=AF.Sqrt, bias=epst[:, 0:1], scale=1.0),
                     reads=[Bst, B("epst")], writes=[Bst])
                S.op("dve", lambda e: e.reciprocal(out=st[:, 24:24 + n], in_=st[:, 20:20 + n]), reads=[Bst], writes=[Bst])
                for i, (buf, Bbuf) in enumerate(items):
                    S.op("dve", (lambda e, buf=buf, i=i: e.scalar_tensor_tensor(out=buf, in0=buf, scalar=st[:, 8 + i:9 + i], in1=lnt[:, gi * 1024:(gi + 1) * 1024],
                                                                                 op0=ALU.subtract, op1=ALU.mult)), reads=[Bbuf, Bst, B("lnt")], writes=[Bbuf])
                    S.op("dve", (lambda e, buf=buf, i=i: e.scalar_tensor_tensor(out=buf, in0=buf, scalar=st[:, 24 + i:25 + i], in1=lnt[:, (gi + 1) * 1024:(gi + 2) * 1024],
                                                                                 op0=ALU.mult, op1=ALU.add)), reads=[Bbuf, Bst, B("lnt")], writes=[Bbuf])

            S.op("dve", lambda e: e.memset(epst[:], EPS), writes=[B("epst")])

            cpt_v = cpt[:].rearrange("p (j f) -> p j f", f=4)

            def stage1(g):
                kbs, N, qoff = ginfo(g)
                nq = N // 128
                xg = x1gs[g % 2]
                for tb in range(nq):
                    t0 = qoff + tb * 128
                    S.op("sp", (lambda e, t0=t0: e.dma_start(out=xblk[:], in_=xo[t0:t0 + 128, :])), writes=[B("xblk")], dma=B("xblk"))
                    for half in range(2):
                        bi = rc["m"] % 3
                        rc["m"] += 1
                        ps = banks[bi]
                        for c in range(8):
                            S.op("pe", (lambda e, ps=ps, c=c, t0=t0, half=half: e.matmul(
                                ps[:, 0:512], lhsT=OT[c][:, t0:t0 + 128], rhs=woutb[:, c * 1024 + half * 512:c * 1024 + (half + 1) * 512],
                                start=(c == 0), stop=(c == 7))), reads=[B("OT", c, g), B("woutb")], writes=[Bbank[bi]])
                        S.op("dve", (lambda e, ps=ps, half=half, tb=tb, xg=xg: e.scalar_tensor_tensor(
                            out=xg[:, tb * 1024 + half * 512:tb * 1024 + (half + 1) * 512], in0=xblk[:, half * 512:(half + 1) * 512], scalar=ALPHA,
                            in1=ps[:, 0:512], op0=ALU.mult, op1=ALU.add)), reads=[Bbank[bi], B("xblk")], writes=[B("x1g", g % 2, tb)])

            def ln1(g):
                kbs, N, qoff = ginfo(g)
                nq = N // 128
                xg = x1gs[g % 2]
                layer_norm([(xg[:, tb * 1024:(tb + 1) * 1024], B("x1g", g % 2, tb)) for tb in range(nq)], 0)

            def transp(g):
                kbs, N, qoff = ginfo(g)
                nq = N // 128
                xg = x1gs[g % 2]
                for tb in range(nq):
                    x1 = xg[:, tb * 1024:(tb + 1) * 1024]
                    S.op("act", (lambda e, x1=x1: e.copy(out=x1b[:], in_=x1)), reads=[B("x1g", g % 2, tb)], writes=[B("x1b")])
                    for dc in range(8):
                        S.op("pe", (lambda e, dc=dc: e.transpose(bankT[:, dc * 128:(dc + 1) * 128], x1b[:, dc * 128:(dc + 1) * 128], identb[:])),
                             reads=[B("x1b"), B("identb")], writes=[BbankT])
                    S.op("act", (lambda e, N=N, tb=tb: e.copy(
                        out=x1T[:, 0:8 * N].rearrange("p (dc t) -> p dc t", dc=8)[:, :, tb * 128:(tb + 1) * 128],
                        in_=bankT[:, 0:1024].rearrange("p (dc t) -> p dc t", dc=8))), reads=[BbankT], writes=[B("x1T")])

            def down_mm(g):
                xg = x1gs[g % 2]
                for hh in range(2):
                    for j in range(NJ):
                        wi = rc["wd"] % NWD
                        rc["wd"] += 1
                        wd = wds[wi]
                        S.op("sp", (lambda e, wd=wd, j=j: e.dma_start(out=wd[:], in_=w_down_b[j * 128:(j + 1) * 128, :])),
                             reads=[B("w_down_b", j)], writes=[B("wd", wi)], dma=B("wd", wi))
                        for tbl in range(2):
                            tb = 2 * hh + tbl
                            for half in range(2):
                                bi = 3 + tbl * 2 + half
                                S.op("pe", (lambda e, bi=bi, wd=wd, j=j, tb=tb, half=half: e.matmul(
                                    banks[bi][:, 0:512], lhsT=hT[:, j * 512 + tb * 128:j * 512 + (tb + 1) * 128],
                                    rhs=wd[:, half * 512:(half + 1) * 512], start=(j == 0), stop=(j == NJ - 1))),
                                    reads=[B("hT", j), B("wd", wi)], writes=[Bbank[bi]])
                    for tbl in range(2):
                        tb = 2 * hh + tbl
                        for half in range(2):
                            bi = 3 + tbl * 2 + half
                            sl = slice(tb * 1024 + half * 512, tb * 1024 + (half + 1) * 512)
                            S.op("dve", (lambda e, bi=bi, sl=sl, xg=xg: e.scalar_tensor_tensor(
                                out=xg[:, sl], in0=xg[:, sl], scalar=ALPHA, in1=banks[bi][:, 0:512], op0=ALU.mult, op1=ALU.add)),
                                reads=[Bbank[bi], B("x1g", g % 2, tb)], writes=[B("x1g", g % 2, tb)])

            def ln2_store(g, hh):
                xg = x1gs[g % 2]
                tbs = (2 * hh, 2 * hh + 1)
                layer_norm([(xg[:, tb * 1024:(tb + 1) * 1024], B("x1g", g % 2, tb)) for tb in tbs], 2)
                for tb in tbs:
                    r0 = (g - 1) * 512 + tb * 128
                    S.op("pool", (lambda e, xg=xg, tb=tb, r0=r0: e.dma_start(out=out[r0:r0 + 128, :], in_=xg[:, tb * 1024:(tb + 1) * 1024])),
                         reads=[B("x1g", g % 2, tb)], dma=B("x1g_st", g % 2, tb))
                    fin_bufs.append(B("x1g_st", g % 2, tb))

            def up(g, hooks):
                kbs, N, qoff = ginfo(g)
                tl_r = tailss[(g - 1) % 2]
                tl_w = tailss[g % 2]
                Btr, Btw = B("tails", (g - 1) % 2), B("tails", g % 2)
                if g >= 1:
                    tr_v = tl_r[:].rearrange("p (j t) -> p j t", t=2)
                    cv = carry[:].rearrange("p (j t) -> p j t", t=2)
                    S.op("dve", lambda e: e.tensor_tensor(out=cv[:, :, 0], in0=tr_v[:, :, 1], in1=cpt_v[:, :, 1], op=ALU.mult),
                         reads=[Btr, B("cpt")], writes=[B("carry")])
                    S.op("dve", lambda e: e.tensor_tensor(out=cv[:, :, 1], in0=tr_v[:, :, 0], in1=cpt_v[:, :, 0], op=ALU.mult),
                         reads=[Btr, B("cpt")], writes=[B("carry")])
                    S.op("dve", lambda e: e.tensor_tensor(out=cv[:, :, 0], in0=cv[:, :, 0], in1=cv[:, :, 1], op=ALU.add),
                         reads=[B("carry")], writes=[B("carry")])
                    S.op("dve", lambda e: e.tensor_tensor(out=cv[:, :, 1], in0=tr_v[:, :, 1], in1=cpt_v[:, :, 0], op=ALU.mult),
                         reads=[Btr, B("cpt"), B("carry")], writes=[B("carry")])
                for j in range(NJ):
                    wi = rc["wu"] % 3
                    rc["wu"] += 1
                    wu = wus[wi]
                    S.op("sp", (lambda e, wu=wu, j=j: e.dma_start(out=wu[:].rearrange("p (dc n) -> p dc n", dc=8), in_=w_up_b[j])),
                         reads=[B("w_up_b", j)], writes=[B("wu", wi)], dma=B("wu", wi))
                    ui = rc["up"] % 2
                    rc["up"] += 1
                    bv, bg = 3 + 2 * ui, 4 + 2 * ui
                    for (bi, off) in ((bv, 0), (bg, 128)):
                        ps = banks[bi]
                        for dc in range(8):
                            S.op("pe", (lambda e, ps=ps, wu=wu, dc=dc, off=off, N=N: e.matmul(
                                ps[:, 0:N], lhsT=wu[:, dc * 256 + off:dc * 256 + off + 128], rhs=x1T[:, dc * N:(dc + 1) * N],
                                start=(dc == 0), stop=(dc == 7))), reads=[B("wu", wi), B("x1T")], writes=[Bbank[bi]])
                    if g == 0:
                        S.op("dve", (lambda e, j=j, bv=bv, N=N: e.tensor_scalar(out=tl_w[:, j * 2:j * 2 + 2], in0=banks[bv][:, N - 2:N],
                                                                                 scalar1=flagt[:, 0:1], scalar2=None, op0=ALU.mult)),
                             reads=[Bbank[bv], B("flagt")], writes=[Btw])
                        S.op("dve", (lambda e, j=j, bg=bg, N=N: e.tensor_scalar(out=tl_w[:, (NJ + j) * 2:(NJ + j) * 2 + 2], in0=banks[bg][:, N - 2:N],
                                                                                 scalar1=flagt[:, 0:1], scalar2=None, op0=ALU.mult)),
                             reads=[Bbank[bg], B("flagt")], writes=[Btw])
                        continue
                    k = rc["u"] % 2
                    rc["u"] += 1
                    for (bi, jj, acc, accn) in ((bv, j, av[k], "av"), (bg, NJ + j, ag[k], "ag")):
                        Bacc = B(accn, k)
                        ps = banks[bi]
                        w0 = cpt[:, jj * 4:jj * 4 + 1]
                        w1 = cpt[:, jj * 4 + 1:jj * 4 + 2]
                        w2 = cpt[:, jj * 4 + 2:jj * 4 + 3]
                        cb_ = cpt[:, jj * 4 + 3:jj * 4 + 4]
                        S.op("act", (lambda e, acc=acc, ps=ps, w2=w2, cb_=cb_: e.activation(out=acc[:], in_=ps[:, 0:512], func=AF.Identity,
                                                                                          bias=cb_, scale=w2)),
                             reads=[Bbank[bi], B("cpt")], writes=[Bacc])
                        S.op("act", (lambda e, jj=jj, ps=ps: e.copy(out=tl_w[:, jj * 2:jj * 2 + 2], in_=ps[:, 510:512])),
                             reads=[Bbank[bi]], writes=[Btw])
                        S.op("dve", (lambda e, acc=acc, ps=ps, w1=w1: e.scalar_tensor_tensor(out=acc[:, 1:512], in0=ps[:, 0:511], scalar=w1,
                                                                                           in1=acc[:, 1:512], op0=ALU.mult, op1=ALU.add)),
                             reads=[Bbank[bi], B("cpt"), Bacc], writes=[Bacc])
                        S.op("dve", (lambda e, acc=acc, ps=ps, w0=w0: e.scalar_tensor_tensor(out=acc[:, 2:512], in0=ps[:, 0:510], scalar=w0,
                                                                                           in1=acc[:, 2:512], op0=ALU.mult, op1=ALU.add)),
                             reads=[Bbank[bi], B("cpt"), Bacc], writes=[Bacc])
                        S.op("dve", (lambda e, acc=acc, jj=jj: e.tensor_tensor(out=acc[:, 0:2], in0=acc[:, 0:2], in1=carry[:, jj * 2:jj * 2 + 2], op=ALU.add)),
                             reads=[B("carry"), Bacc], writes=[Bacc])
                    S.op("act", (lambda e, k=k: e.activation(out=ag[k][:], in_=ag[k][:], func=AF.Gelu_apprx_tanh)),
                         reads=[B("ag", k)], writes=[B("ag", k)])
                    S.op("dve", (lambda e, k=k, j=j: e.tensor_tensor(out=hT[:, j * 512:(j + 1) * 512], in0=av[k][:], in1=ag[k][:], op=ALU.mult)),
                         reads=[B("av", k), B("ag", k)], writes=[B("hT", j)])
                    if j in hooks:
                        hooks[j]()

            fin_bufs = []
            stage1(0)
            ln1(0)
            transp(0)
            up(0, {})
            for g in range(1, NG):
                stage1(g)
                ln1(g)
                if g >= 2:
                    down_mm(g - 1)
                transp(g)
                hooks = {}
                if g >= 2:
                    hooks = {2: (lambda g=g: ln2_store(g - 1, 0)), 6: (lambda g=g: ln2_store(g - 1, 1))}
                up(g, hooks)
            down_mm(NG - 1)
            ln2_store(NG - 1, 0)
            ln2_store(NG - 1, 1)
            fin = list(dict.fromkeys(fin_bufs)) + [B("dbg", k) for k in dbg_out]
            for b in fin:
                if b.cnt:
                    b.w = ("d", b, b.cnt)
            S.op("sp", None, reads=[b for b in fin if b.cnt])
            S.flush(nc, es)
    return nc


def host_inputs(x, w_in, b_forget, rel_bias, w_out, ln1_g, ln1_b, w_up, conv_w, conv_b, w_down, ln2_g, ln2_b):
    f = np.float32
    x = np.asarray(x, f)
    w_in0, w_out0, w_up0, w_down0 = (np.ascontiguousarray(np.asarray(a, f)[0]) for a in (w_in, w_out, w_up, w_down))
    lnp = np.ascontiguousarray(np.stack([np.asarray(a, f)[0] for a in (ln1_g, ln1_b, ln2_g, ln2_b)]))
    cw = np.asarray(conv_w, f)[0]
    cb = np.asarray(conv_b, f)[0]
    convp = np.zeros((128, 44, 4), f)
    for j in range(44):
        base = j * 128 if j < NJ else DFF + (j - NJ) * 128
        convp[:, j, 0:3] = cw[:, base:base + 128].T
        convp[:, j, 3] = cb[base:base + 128]
    convp = np.ascontiguousarray(convp.reshape(128, 176))
    bfgv = np.ascontiguousarray(np.tile(np.asarray(b_forget, f)[0], NB)[None, :])
    rb = np.asarray(rel_bias, f)[0]
    kk = np.arange(128)[:, None]
    qq = np.arange(128)[None, :]
    tiles = np.zeros((128, 8, 5, 128), f)
    for o in range(5):
        idx = np.clip(qq - kk + 128 * o, -128, 128) + 128
        t = rb[:, idx]
        t = np.transpose(t, (1, 0, 2)).copy()
        if o == 0:
            t[:, :, :][np.broadcast_to(((kk >= 64) & (qq < 64))[:, None, :], t.shape)] = NEG
        if o == 4:
            t[np.broadcast_to(((kk < 64) & (qq >= 64))[:, None, :], t.shape)] = NEG
        tiles[:, :, o, :] = t
    biasB = np.ascontiguousarray(tiles.reshape(128, 8 * 5 * 128))
    tri = np.where(kk <= qq, 0.0, NEG).astype(f)
    ident = np.eye(128, dtype=f)
    utri = (kk <= qq).astype(f)
    cst = np.ascontiguousarray(np.concatenate([tri, ident, utri], axis=1))
    maps = []
    for core in range(8):
        b, hi = core // 2, core % 2
        own = x[b, hi * HALF:(hi + 1) * HALF]
        prev = x[b, 0:HALF] if hi else own
        store = np.concatenate([prev, own], axis=0)
        xTm = np.ascontiguousarray(store.T)
        xom = np.ascontiguousarray(store[31 * 128:])
        mA = np.zeros((NG, 64), f)
        mB = np.zeros((NG, 8), f)
        if not hi:
            for g in range(NG):
                kbs, N, _ = ginfo(g)
                lim = 31 if g == 0 else 32
                mA[g, :lim] = NEG
                for st in range(8):
                    if kbs - 4 + st < lim:
                        mB[g, st] = NEG
        maps.append({"xT": xTm, "xo": xom, "w_in": w_in0, "w_out": w_out0, "w_up": w_up0, "w_down": w_down0,
                     "lnp": lnp, "convp": convp, "bfg": bfgv, "biasB": biasB,
                     "maskA": np.ascontiguousarray(mA.reshape(1, -1)), "maskB": np.ascontiguousarray(mB.reshape(1, -1)),
                     "flag": np.full((1, 1), float(hi), f), "cst": cst})
    return maps


_NC = None


def kernel(**inputs):
    global _NC
    maps = host_inputs(**inputs)
    if _NC is None:
        _NC = build_nc()
    res = run_bass_kernel_spmd(_NC, maps, core_ids=list(range(8)))
    outp = np.empty((4, SEQ, D), np.float32)
    for core in range(8):
        b, hi = core // 2, core % 2
        outp[b, hi * HALF:(hi + 1) * HALF] = np.asarray(res.results[core]["out"], np.float32)
    kernel.last_results = res
    return outp
```
